# Optimizing a Trainium2 kernel written in Bass

```python
import math
import jax, jax.numpy as jnp
from jax import lax
import numpy as np

D_MODEL = 1024
BATCH = 2
SEQ = 8192
DEPTH = 2

HEAD_DIM = 64
N_HEADS_DSWA = 8
DSWA_GROUPS = ((128, 1), (512, 4), (2048, 16))
N_HEADS_DIFF = 4
DIFF_VDIM = 2 * HEAD_DIM
WIDTH_DSWA = N_HEADS_DSWA * HEAD_DIM
WIDTH_DIFF = N_HEADS_DIFF * DIFF_VDIM
MIX_WIDTH = WIDTH_DSWA + WIDTH_DIFF
IN_SPLITS = (WIDTH_DSWA,) * 4 + (WIDTH_DIFF,) * 4
IN_COLS = sum(IN_SPLITS)
ROPE_THETA = 500000.0
ROPE_DIM = HEAD_DIM // 4
PLE_DIM = 256
Q_BLOCK = 128
RMS_EPS = 1e-6
SUBLN_EPS = 1e-5

kernel_name = "hymba_dswa_diffattn_ple"


def rmsnorm(x, g, eps=RMS_EPS):
    xf = x.astype(jnp.float32)
    y = xf * lax.rsqrt(jnp.mean(xf * xf, axis=-1, keepdims=True) + eps)
    return (y * g.astype(jnp.float32)).astype(x.dtype)


def rope_partial(x, pos):
    half = ROPE_DIM // 2
    inv = jnp.power(ROPE_THETA, -jnp.arange(half, dtype=jnp.float32) * (2.0 / ROPE_DIM))
    ang = pos.astype(jnp.float32)[:, None] * inv[None, :]
    cos, sin = jnp.cos(ang), jnp.sin(ang)
    xr = x[..., :ROPE_DIM].astype(jnp.float32)
    x1, x2 = xr[..., :half], xr[..., half:]
    rot = jnp.concatenate([x1 * cos - x2 * sin, x2 * cos + x1 * sin], axis=-1).astype(x.dtype)
    return jnp.concatenate([rot, x[..., ROPE_DIM:]], axis=-1)


def dilated_window_group(q, k, v, window, dilation):
    b, h, s, dh = q.shape
    w = window // dilation
    unit = w * dilation
    sp = -(-s // unit) * unit
    L = sp // dilation
    nb = L // w

    def to_blocks(t):
        t = jnp.pad(t, ((0, 0), (0, 0), (0, sp - s), (0, 0)))
        t = t.reshape(b, h, L, dilation, dh).swapaxes(2, 3)
        return t.reshape(b, h, dilation, nb, w, dh)

    def with_prev(t):
        prev = jnp.pad(t[:, :, :, :-1], ((0, 0), (0, 0), (0, 0), (1, 0), (0, 0), (0, 0)))
        return jnp.concatenate([prev, t], axis=4)

    qb = to_blocks(q)
    kc = with_prev(to_blocks(k))
    vc = with_prev(to_blocks(v))
    sc = jnp.einsum('bhrnqd,bhrnkd->bhrnqk', qb, kc).astype(jnp.float32) * (dh ** -0.5)
    i = jnp.arange(w)[:, None]
    j = jnp.arange(2 * w)[None, :]
    dist = w + i - j
    band = (dist >= 0) & (dist <= w)
    mask = band[None] & ((jnp.arange(nb)[:, None, None] > 0) | (j >= w)[None])
    sc = jnp.where(mask, sc, -jnp.inf)
    m = jnp.max(sc, axis=-1, keepdims=True)
    e = jnp.exp(sc - m)
    den = jnp.sum(e, axis=-1)
    o = jnp.einsum('bhrnqk,bhrnkd->bhrnqd', e, vc.astype(jnp.float32)) / den[..., None]
    lse = m[..., 0] + jnp.log(den)

    def from_blocks(t):
        rest = t.shape[5:]
        t = t.reshape((b, h, dilation, L) + rest).swapaxes(2, 3)
        return t.reshape((b, h, sp) + rest)[:, :, :s]

    return from_blocks(o), from_blocks(lse)


def dilated_attention(q, k, v):
    outs, lses = [], []
    for window, dilation in DSWA_GROUPS:
        o, lse = dilated_window_group(q, k, v, window, dilation)
        outs.append(o)
        lses.append(lse)
    wts = jax.nn.softmax(jnp.stack(lses, axis=0), axis=0)
    return jnp.sum(wts[..., None] * jnp.stack(outs, axis=0), axis=0).astype(q.dtype)


def diff_attention(q, k, v, lam, lambda_init, subln_gain):
    b, h, _, s, dh = q.shape
    nq = s // Q_BLOCK
    qb = q.reshape(b, h, 2, nq, Q_BLOCK, dh).transpose(3, 0, 1, 2, 4, 5)
    kpos = jnp.arange(s)

    def block(args):
        qblk, idx = args
        qpos = idx * Q_BLOCK + jnp.arange(Q_BLOCK)
        sc = jnp.einsum('bhcqd,bhckd->bhcqk', qblk, k).astype(jnp.float32) * (dh ** -0.5)
        causal = kpos[None, :] <= qpos[:, None]
        pr = jax.nn.softmax(jnp.where(causal, sc, -jnp.inf), axis=-1)
        a = pr[:, :, 0] - lam * pr[:, :, 1]
        return jnp.einsum('bhqk,bhkd->bhqd', a.astype(v.dtype), v)

    o = lax.map(block, (qb, jnp.arange(nq)))
    o = o.transpose(1, 2, 0, 3, 4).reshape(b, h, s, DIFF_VDIM)
    return rmsnorm(o, subln_gain, SUBLN_EPS) * (1.0 - lambda_init)


def hybrid_layer(h, p_i, layer_idx, norm_gain, w_in, w_out, lq1, lk1, lq2, lk2,
                 subln_gain, ple_norm_gain, w_ple_gate, w_ple):
    b, s, _ = h.shape
    pos = jnp.arange(s)
    u = rmsnorm(h, norm_gain) @ w_in
    qa, ka, va, ga, qd, kd, vd, gd = jnp.split(u, list(np.cumsum(IN_SPLITS)[:-1]), axis=-1)

    def heads_a(t):
        return t.reshape(b, s, N_HEADS_DSWA, HEAD_DIM).transpose(0, 2, 1, 3)
    qa_h = rope_partial(heads_a(qa), pos)
    ka_h = rope_partial(heads_a(ka), pos)
    oa = dilated_attention(qa_h, ka_h, heads_a(va))
    oa = oa.transpose(0, 2, 1, 3).reshape(b, s, WIDTH_DSWA)

    def heads_b(t):
        return t.reshape(b, s, N_HEADS_DIFF, 2, HEAD_DIM).transpose(0, 2, 3, 1, 4)
    qd_h = rope_partial(heads_b(qd), pos)
    kd_h = rope_partial(heads_b(kd), pos)
    vd_h = vd.reshape(b, s, N_HEADS_DIFF, DIFF_VDIM).transpose(0, 2, 1, 3)
    lambda_init = 0.8 - 0.6 * math.exp(-0.3 * layer_idx)
    lam = (jnp.exp(jnp.sum(lq1.astype(jnp.float32) * lk1.astype(jnp.float32)))
           - jnp.exp(jnp.sum(lq2.astype(jnp.float32) * lk2.astype(jnp.float32)))
           + lambda_init)
    od = diff_attention(qd_h, kd_h, vd_h, lam, lambda_init, subln_gain)
    od = od.transpose(0, 2, 1, 3).reshape(b, s, WIDTH_DIFF)

    y = jnp.concatenate([oa * jax.nn.silu(ga), od * jax.nn.silu(gd)], axis=-1) @ w_out
    h = h + y

    gate = jax.nn.sigmoid(rmsnorm(h, ple_norm_gain) @ w_ple_gate)
    return h + (p_i @ w_ple) * gate


def setup_inputs(seed: int = 0) -> dict:
    key = jax.random.key(seed)
    ks = jax.random.split(key, 16)
    f32 = jnp.float32
    nrm = lambda k, shape, scale: jax.random.normal(k, shape, f32) * scale
    return {
        "x": nrm(ks[0], (BATCH, SEQ, D_MODEL), 1.0),
        "p": nrm(ks[1], (DEPTH, BATCH, SEQ, PLE_DIM), 1.0),
        "attn_norm_gain": 1.0 + nrm(ks[2], (DEPTH, D_MODEL), 0.02),
        "w_in": nrm(ks[3], (DEPTH, D_MODEL, IN_COLS), D_MODEL ** -0.5),
        "w_out": nrm(ks[4], (DEPTH, MIX_WIDTH, D_MODEL), MIX_WIDTH ** -0.5),
        "lambda_q1": nrm(ks[5], (DEPTH, HEAD_DIM), 0.1),
        "lambda_k1": nrm(ks[6], (DEPTH, HEAD_DIM), 0.1),
        "lambda_q2": nrm(ks[7], (DEPTH, HEAD_DIM), 0.1),
        "lambda_k2": nrm(ks[8], (DEPTH, HEAD_DIM), 0.1),
        "subln_gain": 1.0 + nrm(ks[9], (DEPTH, DIFF_VDIM), 0.02),
        "ple_norm_gain": 1.0 + nrm(ks[10], (DEPTH, D_MODEL), 0.02),
        "w_ple_gate": nrm(ks[11], (DEPTH, D_MODEL, D_MODEL), D_MODEL ** -0.5),
        "w_ple": nrm(ks[12], (DEPTH, PLE_DIM, D_MODEL), 0.5 * PLE_DIM ** -0.5),
        "final_norm_gain": 1.0 + nrm(ks[13], (D_MODEL,), 0.02),
    }


def reference(x, p, attn_norm_gain, w_in, w_out, lambda_q1, lambda_k1, lambda_q2,
              lambda_k2, subln_gain, ple_norm_gain, w_ple_gate, w_ple, final_norm_gain):
    h = x
    for i in range(DEPTH):
        h = hybrid_layer(h, p[i], i, attn_norm_gain[i], w_in[i], w_out[i],
                         lambda_q1[i], lambda_k1[i], lambda_q2[i], lambda_k2[i],
                         subln_gain[i], ple_norm_gain[i], w_ple_gate[i], w_ple[i])
    return rmsnorm(h, final_norm_gain)
```

```python
import math
import numpy as np
import ml_dtypes
import concourse.bass as bass
import concourse.mybir as mybir
from concourse.bass_utils import run_bass_kernel_spmd

F32 = mybir.dt.float32
BF16 = mybir.dt.bfloat16
AF = mybir.ActivationFunctionType
ALU = mybir.AluOpType
BF = ml_dtypes.bfloat16

D = 1024
SEQ = 8192
NB = 2
DEPTH = 2
TQ = 2048
BLK = 512
NBLK = SEQ // BLK
RMS_EPS = 1e-6
SUBLN_EPS = 1e-5
ROPE_THETA = 500000.0
NDELTA_A = 17
MA_W = 6 * 512
MB_W = 384 + 512

WO_OFF = 190 * 1024
ENGS = ("pe", "act", "dve", "pool", "sp")


class Tok:
    __slots__ = ("kind", "key", "needed", "val")

    def __init__(self, kind, key):
        self.kind = kind
        self.key = key
        self.needed = False
        self.val = 0


class Buf:
    __slots__ = ("w", "r")

    def __init__(self):
        self.w = []
        self.r = []


class Sched:
    def __init__(self):
        self.ops = {e: [] for e in ENGS}
        self.dch = {}
        self.cch = {}

    def _mk(self, q, tok, fn, reads, writes, extra):
        deps = list(extra)
        for b in reads:
            deps += b.w
        for b in writes:
            deps += b.w
            deps += b.r
        for d in deps:
            if d.kind == "E" and d.key == q and q == "pe":
                continue
            d.needed = True
        self.ops[q].append((fn, deps, tok))
        for b in reads:
            if tok.kind == "E":
                b.r = [t for t in b.r if not (t.kind == "E" and t.key == tok.key)]
            b.r.append(tok)
        for b in writes:
            b.w = [tok]
            b.r = []
        return tok

    def op(self, eng, fn, reads=(), writes=(), extra=()):
        return self._mk(eng, Tok("E", eng), fn, reads, writes, extra)

    def dma(self, q, ch, fn, reads=(), writes=(), extra=()):
        t = Tok("D", ch)
        t.needed = True
        self.dch.setdefault(ch, None)
        return self._mk(q, t, fn, reads, writes, extra)

    def cc(self, name, fn, reads=(), writes=(), extra=()):
        t = Tok("C", name)
        t.needed = True
        t.val = 1
        self.cch[name] = None
        return self._mk("pool", t, fn, reads, writes, extra)

    def barrier(self, bufs=()):
        last = []
        for e in ENGS:
            for (_, _, tok) in reversed(self.ops[e]):
                if tok.kind == "E":
                    tok.needed = True
                    last.append(tok)
                    break
        dl = {}
        for e in ENGS:
            for (_, _, tok) in self.ops[e]:
                if tok.kind == "D":
                    dl[tok.key] = tok
        self._barrier_deps = last + list(dl.values())
        return self._barrier_deps

    def assign(self):
        for e in ENGS:
            c = 0
            for (_, _, tok) in self.ops[e]:
                if tok.kind == "E" and tok.needed:
                    c += 1
                    tok.val = c
        dc = {}
        for e in ENGS:
            for (_, _, tok) in self.ops[e]:
                if tok.kind == "D":
                    dc[tok.key] = dc.get(tok.key, 0) + 16
                    tok.val = dc[tok.key]

    def emit(self, eng, engobj, esem, dsem, csem=None):
        waited = {}
        for fn, deps, tok in self.ops[eng]:
            need = {}
            for d in deps:
                if d.kind == "E" and d.key == eng and eng == "pe":
                    continue
                k = (d.kind, d.key)
                if need.get(k, 0) < d.val:
                    need[k] = d.val
            for k, v in need.items():
                if waited.get(k, 0) < v:
                    engobj.wait_ge(esem[k[1]] if k[0] == "E" else (dsem[k[1]] if k[0] == "D" else csem[k[1]]), v)
                    waited[k] = v
            ins = fn(engobj)
            if tok.kind == "D":
                ins.then_inc(dsem[tok.key], 16)
            elif tok.kind == "C":
                ins.then_inc(csem[tok.key])
            elif tok.needed:
                ins.then_inc(esem[eng], 1)


class Arena:
    def __init__(self, t, nbytes):
        self.t = t
        self.n = nbytes
        self.off = 0

    def alloc(self, free, dtype):
        n = 1
        for f in free:
            n *= f
        size = n * (4 if dtype == F32 else 2)
        size = (size + 63) // 64 * 64
        st = self.off
        self.off += size
        assert self.off <= self.n, f"arena overflow {self.off} > {self.n}"
        ap = self.t[:, st // 2:(st + n * (4 if dtype == F32 else 2)) // 2]
        if dtype == F32:
            ap = ap.bitcast(F32)
        if len(free) == 2:
            ap = ap.rearrange("p (a b) -> p a b", a=free[0], b=free[1])
        elif len(free) == 3:
            ap = ap.rearrange("p (a b c) -> p a b c", a=free[0], b=free[1], c=free[2])
        return ap

    def alloc_at(self, off, free, dtype):
        keep = self.off
        self.off = off
        ap = self.alloc(free, dtype)
        end = self.off
        self.off = keep
        return ap, end

    def reset(self):
        self.off = 0


class Builder:
    def __init__(self, phases, fused):
        self.phases = phases
        self.fused = fused
        self.nc = bass.Bass("TRN2", target_bir_lowering=False)
        self.S = Sched()
        self.dram = {}
        self.def_cc = {}
        self.wpre = {}
        self.wo_pre = {}

    def din(self, name, shape, dtype):
        if name not in self.dram:
            self.dram[name] = self.nc.dram_tensor(name, list(shape), dtype, kind="ExternalInput").ap()
        return self.dram[name]

    def dout(self, name, shape, dtype):
        if name not in self.dram:
            self.dram[name] = self.nc.dram_tensor(name, list(shape), dtype, kind="ExternalOutput").ap()
        return self.dram[name]

    def dint(self, name, shape, dtype):
        if name not in self.dram:
            self.dram[name] = self.nc.dram_tensor(name, list(shape), dtype).ap()
        return self.dram[name]

    def norm_block(self, hblk, hbuf, gain_cols, out_tile, out_buf, tmp, eps=RMS_EPS):
        S = self.S
        sq, sqb, ssb, ss_bank, rstd, rstdb, ones = (tmp["sq"], tmp["sqb"], tmp["ssb"], tmp["ss"],
                                                      tmp["rstd"], tmp["rstdb"], tmp["ones"])
        for kc in range(8):
            S.op("act", lambda e, kc=kc: e.activation(out=sq[:, kc, :], in_=hblk[:, kc, :], func=AF.Square),
                 reads=[hbuf], writes=[sqb[kc]])
        for kc in range(8):
            S.op("pe", lambda e, kc=kc: e.matmul(ss_bank, ones, sq[:, kc, :], start=(kc == 0), stop=(kc == 7)),
                 reads=[sqb[kc], self.ctx["cbuf"]], writes=[ssb])
        S.op("act", lambda e: e.activation(out=rstd, in_=ss_bank, func=AF.Ln, scale=1.0 / D, bias=tmp["eps"]),
             reads=[ssb, self.ctx["gbuf"]], writes=[rstdb])
        S.op("act", lambda e: e.activation(out=rstd, in_=rstd, func=AF.Exp, scale=-0.5),
             reads=[rstdb], writes=[rstdb])
        for kc in range(8):
            S.op("dve", lambda e, kc=kc: e.scalar_tensor_tensor(
                out=out_tile[:, kc, :], in0=hblk[:, kc, :], scalar=gain_cols[:, kc:kc + 1], in1=rstd,
                op0=ALU.mult, op1=ALU.mult), reads=[hbuf, rstdb, self.ctx["gbuf"]], writes=[out_buf])

    def build(self):
        nc = self.nc
        S = self.S
        phases = self.phases
        fused = self.fused
        gains = self.din("gains", [128, 48], F32)
        consts_bf = self.din("consts_bf", [128, 128 + 128 + MA_W + MB_W], BF16)
        if "T0" in phases:
            xT = self.din("xT", [D, TQ], F32)
        need_A = any(p.startswith("A") for p in phases)
        need_T = any(p in ("T1", "T2") for p in phases)
        if need_A:
            w_in = self.din("w_in", [DEPTH, D, 1024], F32)
            self.w_in_ap = w_in
            ropeC = self.din("ropeC", [128, SEQ], F32)
            ropeS = self.din("ropeS", [128, SEQ], F32)
            lamv = self.din("lamv", [128, DEPTH * 4 * 64], F32)
        if need_T:
            w_out = self.din("w_out", [DEPTH, D, D], F32)
            self.w_out_ap = w_out
            w_gate = self.din("w_gate", [DEPTH, D, D], F32)
            w_ple = self.din("w_ple", [DEPTH, 256, D], F32)
            pT = self.din("pT", [DEPTH, 256, TQ], F32)
        if fused:
            self.xp = [[self.dint(f"xp{l}_{tb}", [D, BLK], BF16) for tb in range(4)] for l in range(DEPTH)]
            self.xg = [[self.dint(f"xg{l}_{tb}", [4 * D, BLK], BF16) for tb in range(4)] for l in range(DEPTH)]
            self.xpb = [[Buf() for tb in range(4)] for l in range(DEPTH)]
            self.xgb = [[Buf() for tb in range(4)] for l in range(DEPTH)]
            self.yp = [[self.dint(f"yp{l}_{sb}", [256, TQ], BF16) for sb in range(4)] for l in range(DEPTH)]
            self.yf = [self.dint(f"yf{l}", [32, 128, TQ], BF16) for l in range(DEPTH)]
            self.ypb = [[Buf() for sb in range(4)] for l in range(DEPTH)]
            self.yfb = [[Buf() for sb in range(4)] for l in range(DEPTH)]
            self.h_sp = self.dint("h_spill", [D, TQ], F32)
            self.hspb = [Buf() for _ in range(4)]
        outT = None

        with (
            nc.sbuf_tensor("arena", [128, 103 * 1024], BF16) as arena_t,
            nc.psum_tensor("psall", [128, 8, 512], F32) as psall_t,
        ):
            psall = psall_t[:, :, :]
            banks = [psall_t[:, k, :] for k in range(8)]
            bankb = [Buf() for _ in range(8)]
            ar = Arena(arena_t, 206 * 1024)
            gains_sb = ar.alloc([48], F32)
            gbuf = Buf()
            cbf = ar.alloc([128 + 128 + MA_W + MB_W], BF16)
            cbuf = Buf()
            eps_t = ar.alloc([2], F32)
            S.dma("sp", "c0", lambda e: e.dma_start(out=gains_sb, in_=gains), writes=[gbuf])
            S.dma("sp", "c1", lambda e: e.dma_start(out=cbf, in_=consts_bf), writes=[cbuf])
            S.op("dve", lambda e: e.memset(eps_t[:, 0:1], RMS_EPS), writes=[gbuf])
            S.op("dve", lambda e: e.memset(eps_t[:, 1:2], SUBLN_EPS), writes=[gbuf])
            ones = cbf[:, 0:128]
            swapm = cbf[:, 128:256]
            MA = cbf[:, 256:256 + MA_W]
            MB = cbf[:, 256 + MA_W:256 + MA_W + MB_W]
            base_off = ar.off
            self.ctx = dict(psall=psall, banks=banks, bankb=bankb, ar=ar, gains_sb=gains_sb, gbuf=gbuf, cbuf=cbuf,
                            ones=ones, swapm=swapm, MA=MA, MB=MB, eps_t=eps_t)

            first = True
            for ph in phases:
                ar.off = base_off
                if not first:
                    self.phase_barrier()
                first = False
                if ph == "T0":
                    self.phase_T0(xT)
                elif ph in ("A1", "A2"):
                    l = int(ph[1]) - 1
                    self.phase_A(l, w_in, ropeC, ropeS, lamv)
                elif ph in ("T1", "T2"):
                    l = int(ph[1]) - 1
                    self.phase_T(l, w_out, w_gate, w_ple, pT)

            S.assign()
            import contextlib
            with contextlib.ExitStack() as st:
                esem = {e: st.enter_context(nc.semaphore("es_" + e)) for e in ENGS}
                dsem = {ch: st.enter_context(nc.semaphore("ds_" + ch)) for ch in S.dch}
                csem = {ch: st.enter_context(nc.semaphore("cs_" + ch)) for ch in S.cch}
                block = st.enter_context(nc.Block())
                final = []
                for e in ENGS:
                    for (_, _, tok) in S.ops[e]:
                        if tok.kind == "D":
                            final.append(tok)
                lastd = {}
                for t in final:
                    lastd[t.key] = t

                @block.tensor
                def _(eng):
                    S.emit("pe", eng, esem, dsem, csem)

                @block.scalar
                def _(eng):
                    S.emit("act", eng, esem, dsem, csem)

                @block.vector
                def _(eng):
                    S.emit("dve", eng, esem, dsem, csem)

                @block.gpsimd
                def _(eng):
                    S.emit("pool", eng, esem, dsem, csem)

                @block.sync
                def _(eng):
                    S.emit("sp", eng, esem, dsem, csem)
                    for ch, t in lastd.items():
                        eng.wait_ge(dsem[ch], t.val)
        return nc

    def phase_barrier(self):
        S = self.S
        deps = S.barrier()
        for e in ENGS:
            if e == "sp":
                pass
            self._pending_barrier = deps
        self.bar = deps

    def bdeps(self):
        return getattr(self, "bar", [])

    GROUPS = [[0, 1, 2, 3], [4, 5, 6, 7]]

    def pidj512(self, e):
        if getattr(self, "_pidj512", None) is None:
            self._pidj512 = (e.partition_id() % 4) * BLK
        return self._pidj512

    def emit_xn_out(self, lnext, tb, xt, xb):
        S = self.S
        sl = slice(tb * BLK, (tb + 1) * BLK)
        if not self.fused:
            xov = self.dout("xn_part", [D, TQ], BF16).rearrange("(k p) t -> p k t", p=128)
            S.dma("sp", f"xo{tb % 2}", lambda e: e.dma_start(out=xov[:, :, sl], in_=xt), reads=[xb])
            return
        xp, xg = self.xp[lnext][tb], self.xg[lnext][tb]
        xpb, xgb = self.xpb[lnext][tb], self.xgb[lnext][tb]
        S.dma("sp", f"xo{tb % 2}", lambda e: e.dma_start(out=xp.rearrange("(k p) t -> p k t", p=128), in_=xt),
              reads=[xb], writes=[xpb])
        def issue():
            S.cc(f"xg{lnext}_{tb}", lambda e: e.collective_compute(
                "AllGather", ALU.bypass, replica_groups=self.GROUPS, ins=[xp], outs=[xg]), reads=[xpb], writes=[xgb])
        if tb == 0:
            issue()
        else:
            self.def_cc.setdefault(lnext, {})[tb] = issue

    def emit_h_out(self, tb, h, hbuf):
        S = self.S
        sl = slice(tb * BLK, (tb + 1) * BLK)
        if self.fused:
            hov = self.h_sp.rearrange("(k p) t -> p k t", p=128)
            S.dma("sp", f"ho{tb}", lambda e: e.dma_start(out=hov[:, :, sl], in_=h[:, :, sl]),
                  reads=[hbuf], writes=[self.hspb[tb]])
        else:
            hov = self.dout("h_out", [D, TQ], F32).rearrange("(k p) t -> p k t", p=128)
            S.dma("sp", f"ho{tb}", lambda e: e.dma_start(out=hov[:, :, sl], in_=h[:, :, sl]), reads=[hbuf])

    def phase_T0(self, xT):
        S = self.S
        c = self.ctx
        ar = c["ar"]
        banks, bankb = c["banks"], c["bankb"]
        wpre = ar.alloc([8, 1024], BF16)
        h = ar.alloc([8, TQ], F32)
        hb = [Buf() for _ in range(4)]
        tmp = self.norm_tmp(ar)
        xn = [ar.alloc([8, BLK], BF16) for _ in range(2)]
        xnb = [Buf() for _ in range(2)]
        xTv = xT.rearrange("(k p) t -> p k t", p=128)
        bd = self.bdeps()
        ld = {}

        def load_x(tb, after=()):
            sl = slice(tb * BLK, (tb + 1) * BLK)
            ld[tb] = S.dma("sp", f"hld{tb}", lambda e: e.dma_start(out=h[:, :, sl], in_=xTv[:, :, sl]),
                           writes=[hb[tb]], extra=bd + list(after))

        load_x(0)
        if self.fused:
            self.wpre[0] = Buf()
            S.dma("pool", "w", lambda e: e.dma_start(out=wpre, in_=self.w_in_ap[0].rearrange("(k p) n -> p k n", p=128)),
                  writes=[self.wpre[0]], extra=[ld[0]])
        load_x(1, [ld[0]])
        for tb in range(4):
            sl = slice(tb * BLK, (tb + 1) * BLK)
            self.norm_block(h[:, :, sl], hb[tb], c["gains_sb"][:, 0:8], xn[tb % 2], xnb[tb % 2], tmp)
            self.emit_xn_out(0, tb, xn[tb % 2], xnb[tb % 2])
            if not self.fused:
                self.emit_h_out(tb, h, hb[tb])
            if tb + 2 < 4:
                load_x(tb + 2)

    def norm_tmp(self, ar, bank=7):
        c = self.ctx
        return dict(sq=ar.alloc([8, BLK], BF16), sqb=[Buf() for _ in range(8)], ssb=c["bankb"][bank],
                    ss=c["banks"][bank], rstd=ar.alloc([BLK], F32), rstdb=Buf(), ones=c["ones"],
                    eps=c["eps_t"][:, 0:1])

    def phase_T(self, l, w_out, w_gate, w_ple, pT):
        S = self.S
        c = self.ctx
        ar = c["ar"]
        banks, bankb = c["banks"], c["bankb"]
        bd = self.bdeps()
        last = (l == DEPTH - 1)
        if self.fused and not last:
            wnext = ar.alloc([8, 1024], BF16)
        h = ar.alloc([8, TQ], F32)
        hb = [Buf() for _ in range(4)]
        wo, wo_end = ar.alloc_at(WO_OFF, [8, D], BF16)
        wg = ar.alloc([8, D], BF16)
        wp = ar.alloc([2, D], BF16)
        wob, wgb, wpb = Buf(), Buf(), Buf()
        pt = ar.alloc([2, TQ], BF16)
        ptb = Buf()
        ytile = [ar.alloc([8, BLK], BF16) for _ in range(2)]
        yb = [Buf() for _ in range(2)]
        hn = [ar.alloc([8, BLK], BF16) for _ in range(2)]
        hnb = [Buf(), Buf()]
        tmps = [self.norm_tmp(ar, bank=6), self.norm_tmp(ar, bank=7)]
        th = [ar.alloc([BLK], F32) for _ in range(2)]
        thb = [Buf(), Buf()]
        t2 = [ar.alloc([BLK], F32) for _ in range(2)]
        t2b = [Buf(), Buf()]
        if last:
            xo = [ar.alloc([8, BLK], F32) for _ in range(1)]
            xob = [Buf()]
        else:
            xo = [ar.alloc([8, BLK], BF16) for _ in range(1)]
            xob = [Buf()]
        if self.fused and self.wo_pre.get(l) is not None:
            wob = self.wo_pre[l]
        else:
            S.dma("pool", "wo", lambda e: e.dma_start(out=wo, in_=w_out[l].rearrange("(k p) n -> p k n", p=128)),
                  writes=[wob], extra=bd)
        if self.fused:
            h_src = self.dram["xT"] if l == 0 else self.h_sp
            yv = self.yf[l].rearrange("a p t -> p a t")
        else:
            h_src = self.din("h_in", [D, TQ], F32)
            y_full = self.din("y_full", [4 * 256, TQ], BF16)
            yv = y_full.rearrange("(k p) t -> p k t", p=128)
        hsv = h_src.rearrange("(k p) t -> p k t", p=128)

        def load_y(tb, after=()):
            sl = slice(tb * BLK, (tb + 1) * BLK)
            yt, ybuf = ytile[tb % 2], yb[tb % 2]
            if self.fused:
                return S.dma("sp", f"yl{tb % 2}", lambda e: e.dma_start(
                    out=yt, in_=yv[:, 8 * tb:8 * tb + 8, bass.ds(self.pidj512(e), BLK)]),
                    reads=[self.yfb[l][tb]], writes=[ybuf], extra=bd + list(after))
            else:
                S.dma("sp", f"yl{tb % 2}", lambda e: e.dma_start(out=yt, in_=yv[:, :, sl]), writes=[ybuf], extra=bd)

        def load_h(tb, after=()):
            sl = slice(tb * BLK, (tb + 1) * BLK)
            return S.dma("sp", f"hld{tb}", lambda e: e.dma_start(out=h[:, :, sl], in_=hsv[:, :, sl]),
                         reads=([self.hspb[tb]] if (self.fused and l > 0) else []), writes=[hb[tb]],
                         extra=bd + list(after))

        ty0 = load_y(0)
        th0 = load_h(0)
        first = [t_ for t_ in (ty0, th0) if t_ is not None]
        S.dma("pool", "wg", lambda e: e.dma_start(out=wg, in_=w_gate[l].rearrange("(k p) n -> p k n", p=128)),
              writes=[wgb], extra=bd + first)
        load_y(1, first)
        th1 = load_h(1, first)
        S.dma("pool", "wp", lambda e: e.dma_start(out=wp, in_=w_ple[l].rearrange("(k p) n -> p k n", p=128)),
              writes=[wpb], extra=bd + first)
        S.dma("pool", "pt", lambda e: e.dma_start(out=pt, in_=pT[l].rearrange("(k p) t -> p k t", p=128)),
              writes=[ptb], extra=bd + first)
        th2 = load_h(2, [th1])
        th3 = load_h(3, [th2])
        if self.fused and not last:
            self.wpre[l + 1] = Buf()
            S.dma("pool", "w", lambda e: e.dma_start(out=wnext, in_=self.w_in_ap[l + 1].rearrange("(k p) n -> p k n", p=128)),
                  writes=[self.wpre[l + 1]], extra=bd + [th3])
        assert ar.off <= WO_OFF, ar.off
        if last:
            outT = self.dout("outT", [D, TQ], F32)
            ov = outT.rearrange("(k p) t -> p k t", p=128)
        gs = c["gains_sb"]

        def stA(tb):
            sl = slice(tb * BLK, (tb + 1) * BLK)
            yt, ybuf = ytile[tb % 2], yb[tb % 2]
            for dc in range(8):
                bk = dc % 2
                for kc in range(8):
                    S.op("pe", lambda e, dc=dc, kc=kc, bk=bk: e.matmul(
                        banks[bk], wo[:, kc, dc * 128:(dc + 1) * 128], yt[:, kc, :], start=(kc == 0), stop=(kc == 7)),
                        reads=[wob, ybuf], writes=[bankb[bk]])
                S.op("dve", lambda e, dc=dc, bk=bk: e.tensor_tensor(
                    out=h[:, dc, sl], in0=banks[bk], in1=h[:, dc, sl], op=ALU.add),
                    reads=[bankb[bk]], writes=[hb[tb]])
            if tb + 2 < 4:
                load_y(tb + 2)

        def stB(tb):
            sl = slice(tb * BLK, (tb + 1) * BLK)
            self.norm_block(h[:, :, sl], hb[tb], gs[:, 16 + 8 * l:24 + 8 * l], hn[tb % 2], hnb[tb % 2], tmps[0])

        def stC(tb):
            sl = slice(tb * BLK, (tb + 1) * BLK)
            hnt, hnbuf = hn[tb % 2], hnb[tb % 2]
            for dc in range(8):
                bg, be = 2 + (dc % 2), 4 + (dc % 2)
                tht, thbuf, t2t, t2buf = th[dc % 2], thb[dc % 2], t2[dc % 2], t2b[dc % 2]
                for kc in range(8):
                    S.op("pe", lambda e, dc=dc, kc=kc, bg=bg: e.matmul(
                        banks[bg], wg[:, kc, dc * 128:(dc + 1) * 128], hnt[:, kc, :], start=(kc == 0), stop=(kc == 7)),
                        reads=[wgb, hnbuf], writes=[bankb[bg]])
                for kc in range(2):
                    S.op("pe", lambda e, dc=dc, kc=kc, be=be: e.matmul(
                        banks[be], wp[:, kc, dc * 128:(dc + 1) * 128], pt[:, kc, sl], start=(kc == 0), stop=(kc == 1)),
                        reads=[wpb, ptb], writes=[bankb[be]])
                S.op("act", lambda e, bg=bg, tht=tht: e.activation(out=tht, in_=banks[bg], func=AF.Tanh, scale=0.5),
                     reads=[bankb[bg]], writes=[thbuf])
                S.op("dve", lambda e, be=be, tht=tht, t2t=t2t: e.scalar_tensor_tensor(
                    out=t2t, in0=tht, scalar=1.0, in1=banks[be], op0=ALU.add, op1=ALU.mult),
                    reads=[bankb[be], thbuf], writes=[t2buf])
                S.op("dve", lambda e, dc=dc, t2t=t2t: e.scalar_tensor_tensor(
                    out=h[:, dc, sl], in0=t2t, scalar=0.5, in1=h[:, dc, sl], op0=ALU.mult, op1=ALU.add),
                    reads=[t2buf], writes=[hb[tb]])

        def stD(tb):
            sl = slice(tb * BLK, (tb + 1) * BLK)
            gcols = gs[:, 32:40] if last else gs[:, 8 * (l + 1):8 * (l + 2)]
            xt, xb = xo[tb % len(xo)], xob[tb % len(xo)]
            self.norm_block(h[:, :, sl], hb[tb], gcols, xt, xb, tmps[1])
            if last:
                S.dma("sp", f"xo{tb % len(xo)}", lambda e: e.dma_start(out=ov[:, :, sl], in_=xt), reads=[xb])
            else:
                self.emit_xn_out(l + 1, tb, xt, xb)
                self.emit_h_out(tb, h, hb[tb])

        order = ["A0", "A1", "B0", "A2", "B1", "C0", "A3", "B2", "D0", "C1", "B3", "D1", "C2", "D2", "C3", "D3"]
        fmap = {"A": stA, "B": stB, "C": stC, "D": stD}
        for o in order:
            fmap[o[0]](int(o[1]))

    def phase_A(self, l, w_in, ropeC, ropeS, lamv):
        S = self.S
        c = self.ctx
        ar = c["ar"]
        banks, bankb = c["banks"], c["bankb"]
        ones, swapm, MA, MB = c["ones"], c["swapm"], c["MA"], c["MB"]
        cbuf = c["cbuf"]
        bd = self.bdeps()
        gs = c["gains_sb"]
        lambda_init = 0.8 - 0.6 * math.exp(-0.3 * l)

        if self.fused:
            xn_full = None
            y_out = None
        else:
            xn_full = self.din("xn_full", [4 * D, TQ], BF16)
            y_out = self.dout("y_part", [256, SEQ], BF16)

        w = ar.alloc([8, 1024], BF16)
        wb = Buf()
        xnt = [ar.alloc([8, BLK], BF16) for _ in range(2)]
        xnb = [Buf(), Buf()]
        KAT = ar.alloc([SEQ], BF16)
        KBT = ar.alloc([SEQ], BF16)
        VA = ar.alloc([32, 192], BF16)
        V4 = ar.alloc([2, 16, 192], BF16)
        V16 = ar.alloc([2, 16, 192], BF16)
        v4b = [Buf(), Buf()]
        v16b = [Buf(), Buf()]
        vscr = self.dint(f"vscr{l}", [SEQ, 128], BF16)
        vsb = [Buf() for _ in range(NBLK)]
        VB = ar.alloc([64, 128], BF16)
        kab = [Buf() for _ in range(NBLK)]
        kbb = [Buf() for _ in range(NBLK)]
        vab = [Buf() for _ in range(NBLK)]
        vbb = [Buf() for _ in range(NBLK)]
        rC = [ar.alloc([BLK], F32) for _ in range(2)]
        rS = [ar.alloc([BLK], F32) for _ in range(2)]
        rb = [Buf(), Buf()]
        QS = [ar.alloc([TQ], BF16) for _ in range(2)]
        qsb = [[Buf() for _ in range(4)] for _ in range(2)]
        qB = ar.alloc([BLK], BF16)
        qbb = Buf()
        GS = [ar.alloc([TQ], BF16) for _ in range(2)]
        gsb = [[Buf() for _ in range(4)] for _ in range(2)]
        gBs = [ar.alloc([BLK], BF16) for _ in range(2)]
        gbbs = [Buf(), Buf()]
        da = ar.alloc([BLK], F32)
        db = ar.alloc([BLK], F32)
        dcc = ar.alloc([BLK], F32)
        dab, dbb, dcb = Buf(), Buf(), Buf()
        qraw = ar.alloc([BLK], BF16)
        qrb = Buf()
        t1 = ar.alloc([BLK], F32)
        t2 = ar.alloc([BLK], F32)
        t1b, t2b = Buf(), Buf()
        NP = 8
        PP = [ar.alloc([2, BLK], BF16) for _ in range(4)]
        P = [PP[k // 2][:, k % 2, :] for k in range(NP)]
        Pb = [Buf() for _ in range(NP)]
        psall = c["psall"]
        PS = [ar.alloc([2, BLK], BF16) for _ in range(2)]
        psb = [Buf(), Buf()]
        ta, tb_ = da, db
        tc = ar.alloc([BLK], F32)
        tab, tbb, tcb = dab, dbb, Buf()
        sqt = ar.alloc([BLK], BF16)
        sqb = Buf()
        YA = ar.alloc([TQ], BF16)
        yasb = Buf()
        yB = [ar.alloc([BLK], BF16) for _ in range(2)]
        ybb = [Buf(), Buf()]
        lam_t = ar.alloc([4 * 64], F32)
        lam_s = ar.alloc([8], F32)
        lamb = Buf()

        if self.fused and self.wpre.get(l) is not None:
            wb = self.wpre[l]
        else:
            S.dma("pool", "w", lambda e: e.dma_start(out=w, in_=w_in[l].rearrange("(k p) n -> p k n", p=128)),
                  writes=[wb], extra=bd)
        S.dma("sp", "lam", lambda e: e.dma_start(out=lam_t, in_=lamv[:, l * 256:(l + 1) * 256]),
              writes=[lamb], extra=bd)
        S.op("pool", lambda e: e.memset(VA[:, :, 64:128], 1.0), writes=vab, extra=bd)
        S.op("pool", lambda e: e.memset(V4[:, :, :, 64:128], 1.0), writes=v4b, extra=bd)
        S.op("pool", lambda e: e.memset(V16[:, :, :, 64:128], 1.0), writes=v16b, extra=bd)
        S.op("dve", lambda e: e.tensor_tensor(out=lam_t[:, 0:64], in0=lam_t[:, 0:64], in1=lam_t[:, 64:128], op=ALU.mult),
             reads=[lamb], writes=[lamb], extra=bd)
        S.op("dve", lambda e: e.tensor_tensor(out=lam_t[:, 128:192], in0=lam_t[:, 128:192], in1=lam_t[:, 192:256], op=ALU.mult),
             reads=[lamb], writes=[lamb])
        S.op("dve", lambda e: e.reduce_sum(out=lam_s[:, 0:1], in_=lam_t[:, 0:64], axis=mybir.AxisListType.X),
             reads=[lamb], writes=[lamb])
        S.op("dve", lambda e: e.reduce_sum(out=lam_s[:, 1:2], in_=lam_t[:, 128:192], axis=mybir.AxisListType.X),
             reads=[lamb], writes=[lamb])
        S.op("act", lambda e: e.activation(out=lam_s[:, 2:4], in_=lam_s[:, 0:2], func=AF.Exp),
             reads=[lamb], writes=[lamb])
        S.op("dve", lambda e: e.tensor_tensor(out=lam_s[:, 4:5], in0=lam_s[:, 3:4], in1=lam_s[:, 2:3], op=ALU.subtract),
             reads=[lamb], writes=[lamb])
        S.op("dve", lambda e: e.tensor_scalar(out=lam_s[:, 4:5], in0=lam_s[:, 4:5], scalar1=-lambda_init, scalar2=None,
                                              op0=ALU.add), reads=[lamb], writes=[lamb])
        S.op("dve", lambda e: e.tensor_scalar(out=lam_s[:, 5:6], in0=gs[:, 40 + l:41 + l], scalar1=(1.0 - lambda_init),
                                              scalar2=None, op0=ALU.mult), reads=[lamb, c["gbuf"]], writes=[lamb])
        neglam = lam_s[:, 4:5]
        sgain = lam_s[:, 5:6]
        eps_sub = c["eps_t"][:, 1:2]

        if self.fused:
            xgv = [g.rearrange("(r k p) t -> p r k t", p=128, k=8) for g in self.xg[l]]
        else:
            xv = xn_full.rearrange("(r k p) t -> p r k t", p=128, k=8)
            yov = y_out
        pi = [0]

        def nextP():
            i = pi[0] % NP
            pi[0] += 1
            return P[i], Pb[i]

        qraws = [qraw, ar.alloc([BLK], BF16)]
        qrbs = [qrb, Buf()]
        t1s = [t1, ar.alloc([BLK], F32)]
        t2s = [t2, ar.alloc([BLK], F32)]
        t1bs = [t1b, Buf()]
        t2bs = [t2b, Buf()]

        def rope_cast(bank_i, u):
            S.op("dve", lambda e: e.tensor_copy(out=qraws[u], in_=banks[bank_i]), reads=[bankb[bank_i]], writes=[qrbs[u]])

        def rope_rest(bank_i, dst, dstbuf, rbi, u):
            bq, bqb = banks[bank_i], bankb[bank_i]
            S.op("pe", lambda e: e.matmul(banks[2], swapm, qraws[u], start=True, stop=True),
                 reads=[qrbs[u], cbuf], writes=[bankb[2]])
            S.op("dve", lambda e: e.tensor_tensor(out=t1s[u], in0=bq, in1=rC[rbi], op=ALU.mult),
                 reads=[bqb, rb[rbi]], writes=[t1bs[u]])
            S.op("dve", lambda e: e.tensor_tensor(out=t2s[u], in0=banks[2], in1=rS[rbi], op=ALU.mult),
                 reads=[bankb[2], rb[rbi]], writes=[t2bs[u]])
            S.op("pool", lambda e: e.tensor_tensor(out=dst, in0=t1s[u], in1=t2s[u], op=ALU.add),
                 reads=[t1bs[u], t2bs[u]], writes=[dstbuf])

        def gate_evac(bank_i, dst, dstbuf):
            bq, bqb = banks[bank_i], bankb[bank_i]
            S.op("act", lambda e: e.activation(out=t1, in_=bq, func=AF.Tanh, scale=0.5), reads=[bqb], writes=[t1b])
            S.op("pool", lambda e: e.tensor_scalar(out=t1, in0=t1, scalar1=0.5, scalar2=0.5, op0=ALU.mult, op1=ALU.add),
                 reads=[t1b], writes=[t1b])
            S.op("dve", lambda e: e.tensor_tensor(out=dst, in0=bq, in1=t1, op=ALU.mult),
                 reads=[bqb, t1b], writes=[dstbuf])

        def emit_loads(i):
            xt, xb = xnt[i % 2], xnb[i % 2]
            rbi = i % 2
            sl = slice(i * BLK, (i + 1) * BLK)
            if self.fused:
                S.dma("sp", f"xn{i % 2}", lambda e: e.dma_start(out=xt, in_=xgv[i // 4][:, i % 4, :, :]),
                      reads=[self.xgb[l][i // 4]], writes=[xb], extra=bd)
            else:
                S.dma("sp", f"xn{i % 2}", lambda e: e.dma_start(
                    out=xt, in_=xv[:, i // 4, :, (i % 4) * BLK:(i % 4 + 1) * BLK]), writes=[xb], extra=bd)
            S.dma("sp", f"rc{rbi}", lambda e: e.dma_start(out=rC[rbi], in_=ropeC[:, sl]), writes=[rb[rbi]], extra=bd)
            S.dma("sp", f"rs{rbi}", lambda e: e.dma_start(out=rS[rbi], in_=ropeS[:, sl]), writes=[rb[rbi]], extra=bd)

        def diff_epilogue(i):
            sl = slice(i * BLK, (i + 1) * BLK)
            sbi, qs = i // 4, slice((i % 4) * BLK, (i % 4 + 1) * BLK)
            O0, O1, D0, D1 = banks[4], banks[5], banks[6], banks[7]
            gB, gbb = gBs[i % 2], gbbs[i % 2]
            yb_, ybbuf = yB[i % 2], ybb[i % 2]
            Dz = banks[6]
            S.op("act", lambda e: e.activation(out=dcc, in_=Dz, func=AF.Ln), reads=[bankb[6]], writes=[dcb])
            S.op("act", lambda e: e.activation(out=dcc, in_=dcc, func=AF.Exp, scale=-1.0), reads=[dcb], writes=[dcb])
            for hf in range(2):
                rr = slice(64 * hf, 64 * hf + 64)
                S.op("dve", lambda e, rr=rr: e.tensor_tensor(out=da[rr, :], in0=O0[rr, :], in1=dcc[0:64, :], op=ALU.mult),
                     reads=[bankb[4], dcb], writes=[dab])
                S.op("dve", lambda e, rr=rr: e.tensor_tensor(out=db[rr, :], in0=O1[rr, :], in1=dcc[64:128, :], op=ALU.mult),
                     reads=[bankb[5], dcb], writes=[dbb])
            S.op("dve", lambda e: e.scalar_tensor_tensor(out=da, in0=db, scalar=neglam, in1=da, op0=ALU.mult, op1=ALU.add),
                 reads=[dbb, dab, lamb], writes=[dab])
            S.op("act", lambda e: e.activation(out=sqt, in_=da, func=AF.Square), reads=[dab], writes=[sqb])

        def diff_epilogue2(i):
            sl = slice(i * BLK, (i + 1) * BLK)
            sbi, qs = i // 4, slice((i % 4) * BLK, (i % 4 + 1) * BLK)
            gB, gbb = gBs[i % 2], gbbs[i % 2]
            yb_, ybbuf = yB[i % 2], ybb[i % 2]
            S.op("pe", lambda e: e.matmul(banks[7], ones, sqt, start=True, stop=True), reads=[sqb, cbuf], writes=[bankb[7]])
            S.op("act", lambda e: e.activation(out=dcc, in_=banks[7], func=AF.Ln, scale=1.0 / 128, bias=eps_sub),
                 reads=[bankb[7], c["gbuf"]], writes=[dcb])
            S.op("act", lambda e: e.activation(out=dcc, in_=dcc, func=AF.Exp, scale=-0.5), reads=[dcb], writes=[dcb])
            S.op("dve", lambda e: e.scalar_tensor_tensor(out=da, in0=da, scalar=sgain, in1=dcc, op0=ALU.mult, op1=ALU.mult),
                 reads=[dab, dcb, lamb], writes=[dab])
            S.op("pool", lambda e: e.tensor_tensor(out=yb_, in0=da, in1=gB, op=ALU.mult),
                 reads=[dab, gbb], writes=[ybbuf])
            if self.fused:
                S.dma("sp", f"yb{i % 2}", lambda e: e.dma_start(out=self.yp[l][sbi][128:256, qs], in_=yb_),
                      reads=[ybbuf], writes=[self.ypb[l][sbi]])
            else:
                S.dma("sp", f"yb{i % 2}", lambda e: e.dma_start(out=yov[128:256, sl], in_=yb_), reads=[ybbuf])

        LT4, UT4 = MA[:, 0:512], MA[:, 512:1024]
        M1 = [MA[:, 1024 + 512 * r:1536 + 512 * r] for r in range(4)]

        def dswa_superblock(n):
            par = n % 2
            t0 = TQ * n
            Q, G = QS[par], GS[par]
            qb_all = qsb[par]

            def class_batches(h, r4):
                hr = slice(64 * h, 64 * h + 64)
                vs = slice(64 * h, 64 * h + 128)
                bl = []
                for half in range(2):
                    tiles = []
                    mms = list(range(8)) if half == 0 else list(range(8, 15))
                    for t_, mm in enumerate(mms):
                        m = 16 * n + mm
                        tiles.append(dict(sc=slice(64 * t_, 64 * t_ + 64), k=KAT[hr, 128 * m:128 * m + 128],
                                          q=Q[hr, 128 * mm + r4:128 * mm + 256:4], kb=[kab[m // 4]],
                                          oc=slice(32 * mm, 32 * mm + 64), v=VA[:, m % 32, vs], vb=[vab[m // 4]]))
                    if half == 1:
                        m = 16 * n + 15
                        tiles.append(dict(sc=slice(448, 480), k=KAT[hr, 128 * m:128 * m + 128],
                                          q=Q[hr, 1920 + r4:2048:4], kb=[kab[m // 4]],
                                          oc=slice(480, 512), v=VA[:, m % 32, vs], vb=[vab[m // 4]]))
                        if n > 0:
                            m = 16 * n - 1
                            tiles.append(dict(sc=slice(480, 512), k=KAT[hr, 128 * m:128 * m + 128],
                                              q=Q[hr, r4:128:4], kb=[kab[m // 4]],
                                              oc=slice(0, 32), v=VA[:, m % 32, vs], vb=[vab[m // 4]]))
                    bl.append((M1[r4], tiles))
                for prev in range(2):
                    tiles = []
                    for j in range(4):
                        jj, nn = (j, n) if not prev else ((j - 1, n) if j > 0 else (3, n - 1))
                        if nn < 0:
                            continue
                        ks = TQ * nn + BLK * jj + r4
                        tiles.append(dict(sc=slice(128 * j, 128 * j + 128), k=KAT[hr, ks:ks + 4 * 127 + 1:4],
                                          q=Q[hr, BLK * j + r4:BLK * (j + 1):4], kb=[kab[4 * nn + jj]],
                                          oc=slice(128 * j, 128 * j + 128), v=V4[:, nn % 2, 4 * r4 + jj, vs],
                                          vb=[v4b[nn % 2]]))
                    bl.append((UT4 if prev else LT4, tiles))
                for prev in range(2):
                    nn = n - prev
                    if nn < 0:
                        continue
                    tiles = []
                    for s_ in range(4):
                        r16 = r4 + 4 * s_
                        ks = TQ * nn + r16
                        tiles.append(dict(sc=slice(128 * s_, 128 * s_ + 128), k=KAT[hr, ks:ks + 16 * 127 + 1:16],
                                          q=Q[hr, r16:TQ:16], kb=[kab[4 * nn + q_] for q_ in range(4)],
                                          oc=slice(s_, 512, 4), v=V16[:, nn % 2, r16, vs], vb=[v16b[nn % 2]]))
                    bl.append((UT4 if prev else LT4, tiles))
                return bl

            batches = []
            for r4 in range(4):
                b0, b1 = class_batches(0, r4), class_batches(1, r4)
                for bi in range(len(b0)):
                    batches.append(dict(mask=b0[bi][0], tiles=(b0[bi][1], b1[bi][1]), accs=(4 + 2 * (r4 % 2), 5 + 2 * (r4 % 2)),
                                        first=(bi == 0), last=(bi == len(b0) - 1), r4=r4))
            nb_ = len(batches)

            def e_S(b_):
                B = batches[b_]
                pr = b_ % 2
                for T0_, T1_ in zip(*B["tiles"]):
                    for hh_, T in ((0, T0_), (1, T1_)):
                        S.op("pe", lambda e, T=T, hh_=hh_: e.matmul(banks[2 * pr + hh_][:, T["sc"]], T["k"], T["q"],
                                                                 start=True, stop=True),
                             reads=T["kb"] + qb_all, writes=[bankb[2 * pr + hh_]])

            def e_exp(b_):
                B = batches[b_]
                pr = b_ % 2
                S.op("act", lambda e: e.activation(out=PP[pr], in_=psall[:, 2 * pr:2 * pr + 2, :], func=AF.Exp, scale=0.125),
                     reads=[bankb[2 * pr], bankb[2 * pr + 1]], writes=[Pb[2 * pr], Pb[2 * pr + 1]])
                for hh_ in range(2):
                    S.op("dve", lambda e, hh_=hh_: e.tensor_tensor(out=PP[pr][:, hh_, :], in0=PP[pr][:, hh_, :], in1=B["mask"],
                                                                 op=ALU.mult), reads=[Pb[2 * pr + hh_], cbuf], writes=[Pb[2 * pr + hh_]])

            def e_PV(b_):
                B = batches[b_]
                pr = b_ % 2
                nt = len(B["tiles"][0])
                if B["first"]:
                    for hh_ in range(2):
                        acc, accb = banks[B["accs"][hh_]], bankb[B["accs"][hh_]]
                        S.op("pe", lambda e, acc=acc: e.matmul(acc, MB[:, 0:128], MA[:, 0:512], start=True, stop=False),
                             reads=[cbuf], writes=[accb])
                for ti, (T0_, T1_) in enumerate(zip(*B["tiles"])):
                    for hh_, T in ((0, T0_), (1, T1_)):
                        acc, accb = banks[B["accs"][hh_]], bankb[B["accs"][hh_]]
                        S.op("pe", lambda e, T=T, ti=ti, hh_=hh_, acc=acc: e.matmul(
                            acc[:, T["oc"]], T["v"], PP[pr][:, hh_, T["sc"]], start=False,
                            stop=(B["last"] and ti == nt - 1)), reads=T["vb"] + [Pb[2 * pr + hh_]], writes=[accb])
                if B["last"]:
                    r4 = B["r4"]
                    for h in range(2):
                        acc, accb = banks[B["accs"][h]], bankb[B["accs"][h]]
                        tA, tAb, tB, tBb = (ta, tab, tb_, tbb) if h == 0 else (tc, tcb, dcc, dcb)
                        ro = slice(64 * h, 64 * h + 64)
                        rd = slice(64 * (1 - h), 64 * (1 - h) + 64)
                        S.op("act", lambda e, acc=acc, rd=rd, tA=tA: e.activation(out=tA[rd, :], in_=acc[rd, :], func=AF.Ln),
                             reads=[accb], writes=[tAb])
                        S.op("act", lambda e, rd=rd, tA=tA: e.activation(out=tA[rd, :], in_=tA[rd, :], func=AF.Exp, scale=-1.0),
                             reads=[tAb], writes=[tAb])
                        S.op("dve", lambda e, acc=acc, ro=ro, rd=rd, tA=tA, tB=tB: e.tensor_tensor(
                            out=tB[ro, :], in0=acc[ro, :], in1=tA[rd, :], op=ALU.mult), reads=[accb, tAb], writes=[tBb])
                        S.op("pool", lambda e, ro=ro, tB=tB: e.tensor_tensor(out=YA[ro, r4:TQ:4], in0=tB[ro, :], in1=G[ro, r4:TQ:4],
                                                                           op=ALU.mult), reads=[tBb] + gsb[par], writes=[yasb])

            e_S(0)
            for b_ in range(nb_):
                if b_ + 1 < nb_:
                    e_S(b_ + 1)
                e_exp(b_)
                e_PV(b_)
            if self.fused:
                S.dma("sp", "ya", lambda e: e.dma_start(out=self.yp[l][n][0:128, :], in_=YA),
                      reads=[yasb], writes=[self.ypb[l][n]])
                S.cc(f"yg{l}_{n}", lambda e: e.collective_compute(
                    "AllGather", ALU.bypass, replica_groups=self.GROUPS, ins=[self.yp[l][n]],
                    outs=[self.yf[l][8 * n:8 * n + 8].rearrange("k p t -> (k p) t")]),
                    reads=[self.ypb[l][n]], writes=[self.yfb[l][n]])
            else:
                S.dma("sp", "ya", lambda e: e.dma_start(out=yov[0:128, t0:t0 + TQ], in_=YA), reads=[yasb])

        emit_loads(0)
        for i in range(NBLK):
            xt, xb = xnt[i % 2], xnb[i % 2]
            rbi = i % 2
            sl = slice(i * BLK, (i + 1) * BLK)
            if self.fused and i in (1, 5, 9):
                self.def_cc[l][(i + 3) // 4]()
            if self.fused and i == 10:
                wo_t, _ = ar.alloc_at(WO_OFF, [8, D], BF16)
                self.wo_pre[l] = Buf()
                S.dma("pool", "wo", lambda e: e.dma_start(out=wo_t, in_=self.w_out_ap[l].rearrange("(k p) n -> p k n", p=128)),
                      writes=[self.wo_pre[l]], extra=bd)
            sq_ = slice((i % 4) * BLK, (i % 4 + 1) * BLK)
            spar = (i // 4) % 2

            def proj_group(gc, bk):
                for kc in range(8):
                    S.op("pe", lambda e, kc=kc, xt=xt: e.matmul(
                        banks[bk], w[:, kc, gc * 128:(gc + 1) * 128], xt[:, kc, :], start=(kc == 0), stop=(kc == 7)),
                        reads=[wb, xb], writes=[bankb[bk]])

            def v_group(t, bk):
                kt = 4 * i + t
                for kc in range(8):
                    S.op("pe", lambda e, kc=kc, xt=xt: e.matmul(
                        banks[bk][:, 0:256], xt[:, kc, t * 128:(t + 1) * 128], w[:, kc, 768:1024],
                        start=(kc == 0), stop=(kc == 7)), reads=[wb, xb], writes=[bankb[bk]])
                S.op("dve", lambda e: e.tensor_copy(out=VA[:, kt % 32, 0:64], in_=banks[bk][:, 0:64]),
                     reads=[bankb[bk]], writes=[vab[i]])
                S.op("dve", lambda e: e.tensor_copy(out=VA[:, kt % 32, 128:192], in_=banks[bk][:, 64:128]),
                     reads=[bankb[bk]], writes=[vab[i]])
                S.op("dve", lambda e: e.tensor_copy(out=VB[:, kt, :], in_=banks[bk][:, 128:256]),
                     reads=[bankb[bk]], writes=[vbb[i]])

            ropes = [(0, 0, QS[spar][:, sq_], qsb[spar][i % 4]), (1, 1, KAT[:, sl], kab[i]),
                     (3, 3, qB, qbb), (4, 0, KBT[:, sl], kbb[i])]
            for u_, (gc, bk, dst, dstb) in enumerate(ropes):
                proj_group(gc, bk)
                rope_cast(bk, u_ % 2)
                if u_ > 0:
                    pg, pb_, pd, pdb = ropes[u_ - 1]
                    rope_rest(pb_, pd, pdb, rbi, (u_ - 1) % 2)
            if i > 0:
                diff_epilogue(i - 1)
            v_group(0, 1)
            pg, pb_, pd, pdb = ropes[3]
            rope_rest(pb_, pd, pdb, rbi, 1)
            v_group(1, 3)
            v_group(2, 0)
            v_group(3, 1)
            if i > 0:
                diff_epilogue2(i - 1)
            proj_group(2, 3)
            gate_evac(3, GS[spar][:, sq_], gsb[spar][i % 4])
            proj_group(5, 0)
            gate_evac(0, gBs[i % 2], gbbs[i % 2])

            s0 = (4 * i) % 32
            for hh_ in range(2):
                S.dma("sp", f"vo{hh_}", lambda e, s0=s0, i=i, hh_=hh_: e.dma_start(
                    out=vscr[BLK * i:BLK * (i + 1), 64 * hh_:64 * hh_ + 64].rearrange("(t p) c -> p t c", p=128),
                    in_=VA[:, s0:s0 + 4, 128 * hh_:128 * hh_ + 64]), reads=[vab[i]], writes=[vsb[i]])
            if i % 4 == 3:
                n_sb = i // 4
                vpar = n_sb % 2
                for hh_ in range(2):
                    src16 = vscr[TQ * n_sb:TQ * (n_sb + 1), 64 * hh_:64 * hh_ + 64].rearrange("(p r) c -> p r c", r=16)
                    S.dma("sp", f"v16_{vpar}", lambda e, vpar=vpar, src16=src16, hh_=hh_: e.dma_start(
                        out=V16[:, vpar, :, 128 * hh_:128 * hh_ + 64], in_=src16),
                        reads=[vsb[4 * n_sb + q_] for q_ in range(4)], writes=[v16b[vpar]])
                    for j in range(4):
                        r0 = TQ * n_sb + BLK * j
                        src4 = vscr[r0:r0 + BLK, 64 * hh_:64 * hh_ + 64].rearrange("(p r) c -> p r c", r=4)
                        S.dma("sp", f"v4_{vpar}", lambda e, vpar=vpar, src4=src4, j=j, hh_=hh_: e.dma_start(
                            out=V4[:, vpar, j::4, 128 * hh_:128 * hh_ + 64], in_=src4),
                            reads=[vsb[4 * n_sb + j]], writes=[v4b[vpar]])

            if i + 1 < NBLK:
                emit_loads(i + 1)

            if i % 4 == 0 and i > 0:
                dswa_superblock(i // 4 - 1)

            O0, O1, D0, D1 = banks[4], banks[5], banks[6], banks[7]
            nk = 4 * i + 4

            def d_geo(kt):
                j = kt - 4 * i
                c0 = 128 * max(0, j)
                return j, c0, slice(c0, BLK)

            def d_S(kt):
                par = kt % 2
                j, c0, cs = d_geo(kt)
                ksl = slice(kt * 128, (kt + 1) * 128)
                for comp in range(2):
                    rs_ = slice(64 * comp, 64 * comp + 64)
                    S.op("pe", lambda e, rs_=rs_, comp=comp: e.matmul(
                        banks[2 * par + comp][:, cs], KBT[rs_, ksl], qB[rs_, cs], start=True, stop=True),
                        reads=[kbb[kt // 4], qbb], writes=[bankb[2 * par + comp]])

            def d_exp(kt):
                par = kt % 2
                pq = kt % 4
                j, c0, cs = d_geo(kt)
                S.op("act", lambda e: e.activation(out=PP[pq][:, :, cs], in_=psall[:, 2 * par:2 * par + 2, cs],
                                                   func=AF.Exp, scale=0.125),
                     reads=[bankb[2 * par], bankb[2 * par + 1]], writes=[Pb[2 * pq], Pb[2 * pq + 1]])
                if j >= 0:
                    for comp in range(2):
                        S.op("dve", lambda e, comp=comp: e.tensor_tensor(
                            out=PP[pq][:, comp, c0:c0 + 128], in0=PP[pq][:, comp, c0:c0 + 128], in1=MB[:, 384:512],
                            op=ALU.mult), reads=[Pb[2 * pq + comp], cbuf], writes=[Pb[2 * pq + comp]])

            def d_PV(kt):
                par = kt % 4
                j, c0, cs = d_geo(kt)
                for comp in range(2):
                    Pt, Ptb = P[2 * par + comp], Pb[2 * par + comp]
                    S.op("pe", lambda e, comp=comp, Pt=Pt, nk=nk: e.matmul(
                        banks[4 + comp][:, cs], VB[:, kt, :], Pt[:, cs], start=(kt == 0), stop=(kt == nk - 1)),
                        reads=[vbb[kt // 4], Ptb], writes=[bankb[4 + comp]])

            def d_D(kt):
                par = kt % 2
                pq = kt % 4
                j, c0, cs = d_geo(kt)
                if j >= 0:
                    for comp in range(2):
                        Pt, Ptb = P[2 * pq + comp], Pb[2 * pq + comp]
                        S.op("pe", lambda e, comp=comp, Pt=Pt, nk=nk: e.matmul(
                            banks[6][64 * comp:64 * comp + 64, cs], ones[:, 0:64], Pt[:, cs],
                            start=(kt == 0), stop=(kt == nk - 1), tile_position=(0, 64 * comp)),
                            reads=[cbuf, Ptb], writes=[bankb[6]])
                elif par == 1:
                    q_ = (kt // 2) % 2
                    pa_, pb2 = (kt - 1) % 4, kt % 4
                    S.op("dve", lambda e: e.tensor_tensor(out=PS[q_], in0=PP[pa_], in1=PP[pb2], op=ALU.add),
                         reads=[Pb[2 * pa_], Pb[2 * pa_ + 1], Pb[2 * pb2], Pb[2 * pb2 + 1]], writes=[psb[q_]])
                    for comp in range(2):
                        S.op("pe", lambda e, comp=comp: e.matmul(
                            banks[6][64 * comp:64 * comp + 64, :], ones[:, 0:64], PS[q_][:, comp, :],
                            start=(kt == 1), stop=False, tile_position=(0, 64 * comp)),
                            reads=[cbuf, psb[q_]], writes=[bankb[6]])

            d_S(0)
            for kt in range(nk):
                if kt + 1 < nk:
                    d_S(kt + 1)
                d_exp(kt)
                d_PV(kt)
                d_D(kt)

        diff_epilogue(NBLK - 1)
        diff_epilogue2(NBLK - 1)
        dswa_superblock(NBLK // 4 - 1)


def _tok_idx(j):
    return np.concatenate([np.arange(BLK * (4 * s_ + j), BLK * (4 * s_ + j + 1)) for s_ in range(4)])


def _rope_tables():
    half = 8
    inv = np.power(np.float32(ROPE_THETA), -np.arange(half, dtype=np.float32) * np.float32(2.0 / 16)).astype(np.float32)
    pos = np.arange(SEQ, dtype=np.float32)
    ang = pos[None, :] * inv[:, None]
    cos, sin = np.cos(ang).astype(np.float32), np.sin(ang).astype(np.float32)
    C = np.ones((128, SEQ), np.float32)
    Sg = np.zeros((128, SEQ), np.float32)
    for hb in (0, 64):
        C[hb:hb + 8] = cos
        C[hb + 8:hb + 16] = cos
        Sg[hb:hb + 8] = -sin
        Sg[hb + 8:hb + 16] = sin
    return C, Sg


def _consts_bf():
    ones = np.ones((128, 128), np.float32)
    sw = np.zeros((128, 128), np.float32)
    for m in range(128):
        mm = m % 64
        if mm < 8:
            sw[m + 8, m] = 1.0
        elif mm < 16:
            sw[m - 8, m] = 1.0
    p = np.arange(128)[:, None]
    i_ = np.arange(128)[None, :]
    LT = (i_ >= p).astype(np.float32)
    UT = (p >= i_).astype(np.float32)
    c_ = np.arange(64)[None, :]
    M1 = [((4 * c_ + r - p >= 0) & (4 * c_ + r - p <= 128)).astype(np.float32) for r in range(4)]
    cA = np.concatenate([np.tile(LT, (1, 4)), np.tile(UT, (1, 4))] + [np.tile(m, (1, 8)) for m in M1], axis=1)
    z2 = np.arange(MB_W)[None, :] - 384
    cB = ((z2 - p) >= 0).astype(np.float32)
    return np.concatenate([ones, sw, cA, cB], axis=1).astype(BF)


_PROGS = {}


def _prog(phases, fused):
    key = (tuple(phases), fused)
    if key not in _PROGS:
        b = Builder(list(phases), fused)
        _PROGS[key] = b.build()
    return _PROGS[key]


def _host_prep(x, p, attn_norm_gain, w_in, w_out, lambda_q1, lambda_k1, lambda_q2, lambda_k2, subln_gain,
               ple_norm_gain, w_ple_gate, w_ple, final_norm_gain):
    f = lambda a: np.ascontiguousarray(np.asarray(a, dtype=np.float32))
    x, p = f(x), f(p)
    w_in, w_out, w_ple_gate, w_ple = f(w_in), f(w_out), f(w_ple_gate), f(w_ple)
    C, Sg = _rope_tables()
    cb = _consts_bf()

    def pcol(v):
        return np.asarray(v, np.float32).reshape(8, 128).T

    gains = np.zeros((128, 48), np.float32)
    gains[:, 0:8] = pcol(attn_norm_gain[0])
    gains[:, 8:16] = pcol(attn_norm_gain[1])
    gains[:, 16:24] = pcol(ple_norm_gain[0])
    gains[:, 24:32] = pcol(ple_norm_gain[1])
    gains[:, 32:40] = pcol(final_norm_gain)
    gains[:, 40] = np.asarray(subln_gain[0], np.float32)
    gains[:, 41] = np.asarray(subln_gain[1], np.float32)
    lamv = np.zeros((128, DEPTH * 256), np.float32)
    for l in range(DEPTH):
        for k, v in enumerate((lambda_q1, lambda_k1, lambda_q2, lambda_k2)):
            lamv[:, l * 256 + k * 64:l * 256 + (k + 1) * 64] = np.asarray(v[l], np.float32)[None, :]
    per_core = []
    for c in range(8):
        b, j = c // 4, c % 4
        cols = np.concatenate([np.arange(base + 128 * j, base + 128 * j + 128)
                               for base in (0, 512, 1536, 2048, 2560, 3584, 1024, 3072)])
        rows = np.concatenate([np.concatenate([np.arange(128 * r, 128 * r + 128), np.arange(512 + 128 * r, 512 + 128 * r + 128)])
                               for r in range(4)])
        d = dict(
            gains=gains, consts_bf=cb, ropeC=C, ropeS=Sg, lamv=lamv,
            xT=np.ascontiguousarray(x[b, _tok_idx(j), :].T),
            pT=np.ascontiguousarray(np.transpose(p[:, b, _tok_idx(j), :], (0, 2, 1))),
            w_in=np.ascontiguousarray(w_in[:, :, cols]),
            w_out=np.ascontiguousarray(w_out[:, rows, :]),
            w_gate=w_ple_gate, w_ple=w_ple,
        )
        per_core.append(d)
    return per_core


def _run(phases, fused, in_maps):
    nc = _prog(phases, fused)
    res = run_bass_kernel_spmd(nc, in_maps, core_ids=list(range(8)))
    return res.results


def _sel(d, names):
    return {k: d[k] for k in names}


def kernel_fused(**inputs):
    pc = _host_prep(**inputs)
    names = ["gains", "consts_bf", "xT", "w_in", "ropeC", "ropeS", "lamv", "w_out", "w_gate", "w_ple", "pT"]
    r = _run(["T0", "A1", "T1", "A2", "T2"], True, [_sel(pc[c], names) for c in range(8)])
    out = np.zeros((NB, SEQ, D), np.float32)
    for c in range(8):
        out[c // 4, _tok_idx(c % 4), :] = r[c]["outT"].T
    return out


def kernel(**inputs):
    return kernel_fused(**inputs)
```

```python
import math
import numpy as np
import ml_dtypes
import concourse.bass as bass
import concourse.mybir as mybir
from concourse.bass_utils import run_bass_kernel_spmd

F32 = mybir.dt.float32
BF16 = mybir.dt.bfloat16
AF = mybir.ActivationFunctionType
ALU = mybir.AluOpType
BF = ml_dtypes.bfloat16

D = 1024
SEQ = 8192
NB = 2
DEPTH = 2
TQ = 2048
BLK = 512
NBLK = SEQ // BLK
RMS_EPS = 1e-6
SUBLN_EPS = 1e-5
ROPE_THETA = 500000.0
NDELTA_A = 17
MA_W = 6 * 512
MB_W = 384 + 512

WO_OFF = 190 * 1024
ENGS = ("pe", "act", "dve", "pool", "sp")


class Tok:
    __slots__ = ("kind", "key", "needed", "val")

    def __init__(self, kind, key):
        self.kind = kind
        self.key = key
        self.needed = False
        self.val = 0


class Buf:
    __slots__ = ("w", "r")

    def __init__(self):
        self.w = []
        self.r = []


class Sched:
    def __init__(self):
        self.ops = {e: [] for e in ENGS}
        self.dch = {}
        self.cch = {}

    def _mk(self, q, tok, fn, reads, writes, extra):
        deps = list(extra)
        for b in reads:
            deps += b.w
        for b in writes:
            deps += b.w
            deps += b.r
        for d in deps:
            if d.kind == "E" and d.key == q and q == "pe":
                continue
            d.needed = True
        self.ops[q].append((fn, deps, tok))
        for b in reads:
            if tok.kind == "E":
                b.r = [t for t in b.r if not (t.kind == "E" and t.key == tok.key)]
            b.r.append(tok)
        for b in writes:
            b.w = [tok]
            b.r = []
        return tok

    def op(self, eng, fn, reads=(), writes=(), extra=()):
        return self._mk(eng, Tok("E", eng), fn, reads, writes, extra)

    def dma(self, q, ch, fn, reads=(), writes=(), extra=()):
        t = Tok("D", ch)
        t.needed = True
        self.dch.setdefault(ch, None)
        return self._mk(q, t, fn, reads, writes, extra)

    def cc(self, name, fn, reads=(), writes=(), extra=()):
        t = Tok("C", name)
        t.needed = True
        t.val = 1
        self.cch[name] = None
        return self._mk("pool", t, fn, reads, writes, extra)

    def barrier(self, bufs=()):
        last = []
        for e in ENGS:
            for (_, _, tok) in reversed(self.ops[e]):
                if tok.kind == "E":
                    tok.needed = True
                    last.append(tok)
                    break
        dl = {}
        for e in ENGS:
            for (_, _, tok) in self.ops[e]:
                if tok.kind == "D":
                    dl[tok.key] = tok
        self._barrier_deps = last + list(dl.values())
        return self._barrier_deps

    def assign(self):
        for e in ENGS:
            c = 0
            for (_, _, tok) in self.ops[e]:
                if tok.kind == "E" and tok.needed:
                    c += 1
                    tok.val = c
        dc = {}
        for e in ENGS:
            for (_, _, tok) in self.ops[e]:
                if tok.kind == "D":
                    dc[tok.key] = dc.get(tok.key, 0) + 16
                    tok.val = dc[tok.key]

    def emit(self, eng, engobj, esem, dsem, csem=None):
        waited = {}
        for fn, deps, tok in self.ops[eng]:
            need = {}
            for d in deps:
                if d.kind == "E" and d.key == eng and eng == "pe":
                    continue
                k = (d.kind, d.key)
                if need.get(k, 0) < d.val:
                    need[k] = d.val
            for k, v in need.items():
                if waited.get(k, 0) < v:
                    engobj.wait_ge(esem[k[1]] if k[0] == "E" else (dsem[k[1]] if k[0] == "D" else csem[k[1]]), v)
                    waited[k] = v
            ins = fn(engobj)
            if tok.kind == "D":
                ins.then_inc(dsem[tok.key], 16)
            elif tok.kind == "C":
                ins.then_inc(csem[tok.key])
            elif tok.needed:
                ins.then_inc(esem[eng], 1)


class Arena:
    def __init__(self, t, nbytes):
        self.t = t
        self.n = nbytes
        self.off = 0

    def alloc(self, free, dtype):
        n = 1
        for f in free:
            n *= f
        size = n * (4 if dtype == F32 else 2)
        size = (size + 63) // 64 * 64
        st = self.off
        self.off += size
        assert self.off <= self.n, f"arena overflow {self.off} > {self.n}"
        ap = self.t[:, st // 2:(st + n * (4 if dtype == F32 else 2)) // 2]
        if dtype == F32:
            ap = ap.bitcast(F32)
        if len(free) == 2:
            ap = ap.rearrange("p (a b) -> p a b", a=free[0], b=free[1])
        elif len(free) == 3:
            ap = ap.rearrange("p (a b c) -> p a b c", a=free[0], b=free[1], c=free[2])
        return ap

    def alloc_at(self, off, free, dtype):
        keep = self.off
        self.off = off
        ap = self.alloc(free, dtype)
        end = self.off
        self.off = keep
        return ap, end

    def reset(self):
        self.off = 0


class Builder:
    def __init__(self, phases, fused):
        self.phases = phases
        self.fused = fused
        self.nc = bass.Bass("TRN2", target_bir_lowering=False)
        self.S = Sched()
        self.dram = {}
        self.def_cc = {}
        self.wpre = {}
        self.wo_pre = {}

    def din(self, name, shape, dtype):
        if name not in self.dram:
            self.dram[name] = self.nc.dram_tensor(name, list(shape), dtype, kind="ExternalInput").ap()
        return self.dram[name]

    def dout(self, name, shape, dtype):
        if name not in self.dram:
            self.dram[name] = self.nc.dram_tensor(name, list(shape), dtype, kind="ExternalOutput").ap()
        return self.dram[name]

    def dint(self, name, shape, dtype):
        if name not in self.dram:
            self.dram[name] = self.nc.dram_tensor(name, list(shape), dtype).ap()
        return self.dram[name]

    def norm_block(self, hblk, hbuf, gain_cols, out_tile, out_buf, tmp, eps=RMS_EPS):
        S = self.S
        sq, sqb, ssb, ss_bank, rstd, rstdb, ones = (tmp["sq"], tmp["sqb"], tmp["ssb"], tmp["ss"],
                                                      tmp["rstd"], tmp["rstdb"], tmp["ones"])
        for kc in range(8):
            S.op("act", lambda e, kc=kc: e.activation(out=sq[:, kc, :], in_=hblk[:, kc, :], func=AF.Square),
                 reads=[hbuf], writes=[sqb[kc]])
        for kc in range(8):
            S.op("pe", lambda e, kc=kc: e.matmul(ss_bank, ones, sq[:, kc, :], start=(kc == 0), stop=(kc == 7)),
                 reads=[sqb[kc], self.ctx["cbuf"]], writes=[ssb])
        S.op("act", lambda e: e.activation(out=rstd, in_=ss_bank, func=AF.Ln, scale=1.0 / D, bias=tmp["eps"]),
             reads=[ssb, self.ctx["gbuf"]], writes=[rstdb])
        S.op("act", lambda e: e.activation(out=rstd, in_=rstd, func=AF.Exp, scale=-0.5),
             reads=[rstdb], writes=[rstdb])
        for kc in range(8):
            S.op("dve", lambda e, kc=kc: e.scalar_tensor_tensor(
                out=out_tile[:, kc, :], in0=hblk[:, kc, :], scalar=gain_cols[:, kc:kc + 1], in1=rstd,
                op0=ALU.mult, op1=ALU.mult), reads=[hbuf, rstdb, self.ctx["gbuf"]], writes=[out_buf])

    def build(self):
        nc = self.nc
        S = self.S
        phases = self.phases
        fused = self.fused
        gains = self.din("gains", [128, 48], F32)
        consts_bf = self.din("consts_bf", [128, 128 + 128 + MA_W + MB_W], BF16)
        if "T0" in phases:
            xT = self.din("xT", [D, TQ], F32)
        need_A = any(p.startswith("A") for p in phases)
        need_T = any(p in ("T1", "T2") for p in phases)
        if need_A:
            w_in = self.din("w_in", [DEPTH, D, 1024], F32)
            self.w_in_ap = w_in
            ropeC = self.din("ropeC", [128, SEQ], F32)
            ropeS = self.din("ropeS", [128, SEQ], F32)
            lamv = self.din("lamv", [128, DEPTH * 4 * 64], F32)
        if need_T:
            w_out = self.din("w_out", [DEPTH, D, D], F32)
            self.w_out_ap = w_out
            w_gate = self.din("w_gate", [DEPTH, D, D], F32)
            w_ple = self.din("w_ple", [DEPTH, 256, D], F32)
            pT = self.din("pT", [DEPTH, 256, TQ], F32)
        if fused:
            self.xp = [[self.dint(f"xp{l}_{tb}", [D, BLK], BF16) for tb in range(4)] for l in range(DEPTH)]
            self.xg = [[self.dint(f"xg{l}_{tb}", [4 * D, BLK], BF16) for tb in range(4)] for l in range(DEPTH)]
            self.xpb = [[Buf() for tb in range(4)] for l in range(DEPTH)]
            self.xgb = [[Buf() for tb in range(4)] for l in range(DEPTH)]
            self.yp = [[self.dint(f"yp{l}_{sb}", [256, TQ], BF16) for sb in range(4)] for l in range(DEPTH)]
            self.yf = [self.dint(f"yf{l}", [32, 128, TQ], BF16) for l in range(DEPTH)]
            self.ypb = [[Buf() for sb in range(4)] for l in range(DEPTH)]
            self.yfb = [[Buf() for sb in range(4)] for l in range(DEPTH)]
            self.h_sp = self.dint("h_spill", [D, TQ], F32)
            self.hspb = [Buf() for _ in range(4)]
        outT = None

        with (
            nc.sbuf_tensor("arena", [128, 103 * 1024], BF16) as arena_t,
            nc.psum_tensor("psall", [128, 8, 512], F32) as psall_t,
        ):
            psall = psall_t[:, :, :]
            banks = [psall_t[:, k, :] for k in range(8)]
            bankb = [Buf() for _ in range(8)]
            ar = Arena(arena_t, 206 * 1024)
            gains_sb = ar.alloc([48], F32)
            gbuf = Buf()
            cbf = ar.alloc([128 + 128 + MA_W + MB_W], BF16)
            cbuf = Buf()
            eps_t = ar.alloc([2], F32)
            S.dma("sp", "c0", lambda e: e.dma_start(out=gains_sb, in_=gains), writes=[gbuf])
            S.dma("sp", "c1", lambda e: e.dma_start(out=cbf, in_=consts_bf), writes=[cbuf])
            S.op("dve", lambda e: e.memset(eps_t[:, 0:1], RMS_EPS), writes=[gbuf])
            S.op("dve", lambda e: e.memset(eps_t[:, 1:2], SUBLN_EPS), writes=[gbuf])
            ones = cbf[:, 0:128]
            swapm = cbf[:, 128:256]
            MA = cbf[:, 256:256 + MA_W]
            MB = cbf[:, 256 + MA_W:256 + MA_W + MB_W]
            base_off = ar.off
            self.ctx = dict(psall=psall, banks=banks, bankb=bankb, ar=ar, gains_sb=gains_sb, gbuf=gbuf, cbuf=cbuf,
                            ones=ones, swapm=swapm, MA=MA, MB=MB, eps_t=eps_t)

            first = True
            for ph in phases:
                ar.off = base_off
                if not first:
                    self.phase_barrier()
                first = False
                if ph == "T0":
                    self.phase_T0(xT)
                elif ph in ("A1", "A2"):
                    l = int(ph[1]) - 1
                    self.phase_A(l, w_in, ropeC, ropeS, lamv)
                elif ph in ("T1", "T2"):
                    l = int(ph[1]) - 1
                    self.phase_T(l, w_out, w_gate, w_ple, pT)

            S.assign()
            import contextlib
            with contextlib.ExitStack() as st:
                esem = {e: st.enter_context(nc.semaphore("es_" + e)) for e in ENGS}
                dsem = {ch: st.enter_context(nc.semaphore("ds_" + ch)) for ch in S.dch}
                csem = {ch: st.enter_context(nc.semaphore("cs_" + ch)) for ch in S.cch}
                block = st.enter_context(nc.Block())
                final = []
                for e in ENGS:
                    for (_, _, tok) in S.ops[e]:
                        if tok.kind == "D":
                            final.append(tok)
                lastd = {}
                for t in final:
                    lastd[t.key] = t

                @block.tensor
                def _(eng):
                    S.emit("pe", eng, esem, dsem, csem)

                @block.scalar
                def _(eng):
                    S.emit("act", eng, esem, dsem, csem)

                @block.vector
                def _(eng):
                    S.emit("dve", eng, esem, dsem, csem)

                @block.gpsimd
                def _(eng):
                    S.emit("pool", eng, esem, dsem, csem)

                @block.sync
                def _(eng):
                    S.emit("sp", eng, esem, dsem, csem)
                    for ch, t in lastd.items():
                        eng.wait_ge(dsem[ch], t.val)
        return nc

    def phase_barrier(self):
        S = self.S
        deps = S.barrier()
        for e in ENGS:
            if e == "sp":
                pass
            self._pending_barrier = deps
        self.bar = deps

    def bdeps(self):
        return getattr(self, "bar", [])

    GROUPS = [[0, 1, 2, 3], [4, 5, 6, 7]]

    def pidj512(self, e):
        if getattr(self, "_pidj512", None) is None:
            self._pidj512 = (e.partition_id() % 4) * BLK
        return self._pidj512

    def emit_xn_out(self, lnext, tb, xt, xb):
        S = self.S
        sl = slice(tb * BLK, (tb + 1) * BLK)
        if not self.fused:
            xov = self.dout("xn_part", [D, TQ], BF16).rearrange("(k p) t -> p k t", p=128)
            S.dma("sp", f"xo{tb % 2}", lambda e: e.dma_start(out=xov[:, :, sl], in_=xt), reads=[xb])
            return
        xp, xg = self.xp[lnext][tb], self.xg[lnext][tb]
        xpb, xgb = self.xpb[lnext][tb], self.xgb[lnext][tb]
        S.dma("sp", f"xo{tb % 2}", lambda e: e.dma_start(out=xp.rearrange("(k p) t -> p k t", p=128), in_=xt),
              reads=[xb], writes=[xpb])
        def issue():
            S.cc(f"xg{lnext}_{tb}", lambda e: e.collective_compute(
                "AllGather", ALU.bypass, replica_groups=self.GROUPS, ins=[xp], outs=[xg]), reads=[xpb], writes=[xgb])
        if tb == 0:
            issue()
        else:
            self.def_cc.setdefault(lnext, {})[tb] = issue

    def emit_h_out(self, tb, h, hbuf):
        S = self.S
        sl = slice(tb * BLK, (tb + 1) * BLK)
        if self.fused:
            hov = self.h_sp.rearrange("(k p) t -> p k t", p=128)
            S.dma("sp", f"ho{tb}", lambda e: e.dma_start(out=hov[:, :, sl], in_=h[:, :, sl]),
                  reads=[hbuf], writes=[self.hspb[tb]])
        else:
            hov = self.dout("h_out", [D, TQ], F32).rearrange("(k p) t -> p k t", p=128)
            S.dma("sp", f"ho{tb}", lambda e: e.dma_start(out=hov[:, :, sl], in_=h[:, :, sl]), reads=[hbuf])

    def phase_T0(self, xT):
        S = self.S
        c = self.ctx
        ar = c["ar"]
        banks, bankb = c["banks"], c["bankb"]
        wpre = ar.alloc([8, 1024], BF16)
        h = ar.alloc([8, TQ], F32)
        hb = [Buf() for _ in range(4)]
        tmp = self.norm_tmp(ar)
        xn = [ar.alloc([8, BLK], BF16) for _ in range(2)]
        xnb = [Buf() for _ in range(2)]
        xTv = xT.rearrange("(k p) t -> p k t", p=128)
        bd = self.bdeps()
        ld = {}

        def load_x(tb, after=()):
            sl = slice(tb * BLK, (tb + 1) * BLK)
            ld[tb] = S.dma("sp", f"hld{tb}", lambda e: e.dma_start(out=h[:, :, sl], in_=xTv[:, :, sl]),
                           writes=[hb[tb]], extra=bd + list(after))

        load_x(0)
        if self.fused:
            self.wpre[0] = Buf()
            S.dma("pool", "w", lambda e: e.dma_start(out=wpre, in_=self.w_in_ap[0].rearrange("(k p) n -> p k n", p=128)),
                  writes=[self.wpre[0]], extra=[ld[0]])
        load_x(1, [ld[0]])
        for tb in range(4):
            sl = slice(tb * BLK, (tb + 1) * BLK)
            self.norm_block(h[:, :, sl], hb[tb], c["gains_sb"][:, 0:8], xn[tb % 2], xnb[tb % 2], tmp)
            self.emit_xn_out(0, tb, xn[tb % 2], xnb[tb % 2])
            if not self.fused:
                self.emit_h_out(tb, h, hb[tb])
            if tb + 2 < 4:
                load_x(tb + 2)

    def norm_tmp(self, ar, bank=7):
        c = self.ctx
        return dict(sq=ar.alloc([8, BLK], BF16), sqb=[Buf() for _ in range(8)], ssb=c["bankb"][bank],
                    ss=c["banks"][bank], rstd=ar.alloc([BLK], F32), rstdb=Buf(), ones=c["ones"],
                    eps=c["eps_t"][:, 0:1])

    def phase_T(self, l, w_out, w_gate, w_ple, pT):
        S = self.S
        c = self.ctx
        ar = c["ar"]
        banks, bankb = c["banks"], c["bankb"]
        bd = self.bdeps()
        last = (l == DEPTH - 1)
        if self.fused and not last:
            wnext = ar.alloc([8, 1024], BF16)
        h = ar.alloc([8, TQ], F32)
        hb = [Buf() for _ in range(4)]
        wo, wo_end = ar.alloc_at(WO_OFF, [8, D], BF16)
        wg = ar.alloc([8, D], BF16)
        wp = ar.alloc([2, D], BF16)
        wob, wgb, wpb = Buf(), Buf(), Buf()
        pt = ar.alloc([2, TQ], BF16)
        ptb = Buf()
        ytile = [ar.alloc([8, BLK], BF16) for _ in range(2)]
        yb = [Buf() for _ in range(2)]
        hn = [ar.alloc([8, BLK], BF16) for _ in range(2)]
        hnb = [Buf(), Buf()]
        tmps = [self.norm_tmp(ar, bank=6), self.norm_tmp(ar, bank=7)]
        th = [ar.alloc([BLK], F32) for _ in range(2)]
        thb = [Buf(), Buf()]
        t2 = [ar.alloc([BLK], F32) for _ in range(2)]
        t2b = [Buf(), Buf()]
        if last:
            xo = [ar.alloc([8, BLK], F32) for _ in range(1)]
            xob = [Buf()]
        else:
            xo = [ar.alloc([8, BLK], BF16) for _ in range(1)]
            xob = [Buf()]
        if self.fused and self.wo_pre.get(l) is not None:
            wob = self.wo_pre[l]
        else:
            S.dma("pool", "wo", lambda e: e.dma_start(out=wo, in_=w_out[l].rearrange("(k p) n -> p k n", p=128)),
                  writes=[wob], extra=bd)
        if self.fused:
            h_src = self.dram["xT"] if l == 0 else self.h_sp
            yv = self.yf[l].rearrange("a p t -> p a t")
        else:
            h_src = self.din("h_in", [D, TQ], F32)
            y_full = self.din("y_full", [4 * 256, TQ], BF16)
            yv = y_full.rearrange("(k p) t -> p k t", p=128)
        hsv = h_src.rearrange("(k p) t -> p k t", p=128)

        def load_y(tb, after=()):
            sl = slice(tb * BLK, (tb + 1) * BLK)
            yt, ybuf = ytile[tb % 2], yb[tb % 2]
            if self.fused:
                return S.dma("sp", f"yl{tb % 2}", lambda e: e.dma_start(
                    out=yt, in_=yv[:, 8 * tb:8 * tb + 8, bass.ds(self.pidj512(e), BLK)]),
                    reads=[self.yfb[l][tb]], writes=[ybuf], extra=bd + list(after))
            else:
                S.dma("sp", f"yl{tb % 2}", lambda e: e.dma_start(out=yt, in_=yv[:, :, sl]), writes=[ybuf], extra=bd)

        def load_h(tb, after=()):
            sl = slice(tb * BLK, (tb + 1) * BLK)
            return S.dma("sp", f"hld{tb}", lambda e: e.dma_start(out=h[:, :, sl], in_=hsv[:, :, sl]),
                         reads=([self.hspb[tb]] if (self.fused and l > 0) else []), writes=[hb[tb]],
                         extra=bd + list(after))

        ty0 = load_y(0)
        th0 = load_h(0)
        first = [t_ for t_ in (ty0, th0) if t_ is not None]
        S.dma("pool", "wg", lambda e: e.dma_start(out=wg, in_=w_gate[l].rearrange("(k p) n -> p k n", p=128)),
              writes=[wgb], extra=bd + first)
        load_y(1, first)
        th1 = load_h(1, first)
        S.dma("pool", "wp", lambda e: e.dma_start(out=wp, in_=w_ple[l].rearrange("(k p) n -> p k n", p=128)),
              writes=[wpb], extra=bd + first)
        S.dma("pool", "pt", lambda e: e.dma_start(out=pt, in_=pT[l].rearrange("(k p) t -> p k t", p=128)),
              writes=[ptb], extra=bd + first)
        th2 = load_h(2, [th1])
        th3 = load_h(3, [th2])
        if self.fused and not last:
            self.wpre[l + 1] = Buf()
            S.dma("pool", "w", lambda e: e.dma_start(out=wnext, in_=self.w_in_ap[l + 1].rearrange("(k p) n -> p k n", p=128)),
                  writes=[self.wpre[l + 1]], extra=bd + [th3])
        assert ar.off <= WO_OFF, ar.off
        if last:
            outT = self.dout("outT", [D, TQ], F32)
            ov = outT.rearrange("(k p) t -> p k t", p=128)
        gs = c["gains_sb"]

        def stA(tb):
            sl = slice(tb * BLK, (tb + 1) * BLK)
            yt, ybuf = ytile[tb % 2], yb[tb % 2]
            for dc in range(8):
                bk = dc % 2
                for kc in range(8):
                    S.op("pe", lambda e, dc=dc, kc=kc, bk=bk: e.matmul(
                        banks[bk], wo[:, kc, dc * 128:(dc + 1) * 128], yt[:, kc, :], start=(kc == 0), stop=(kc == 7)),
                        reads=[wob, ybuf], writes=[bankb[bk]])
                S.op("dve", lambda e, dc=dc, bk=bk: e.tensor_tensor(
                    out=h[:, dc, sl], in0=banks[bk], in1=h[:, dc, sl], op=ALU.add),
                    reads=[bankb[bk]], writes=[hb[tb]])
            if tb + 2 < 4:
                load_y(tb + 2)

        def stB(tb):
            sl = slice(tb * BLK, (tb + 1) * BLK)
            self.norm_block(h[:, :, sl], hb[tb], gs[:, 16 + 8 * l:24 + 8 * l], hn[tb % 2], hnb[tb % 2], tmps[0])

        def stC(tb):
            sl = slice(tb * BLK, (tb + 1) * BLK)
            hnt, hnbuf = hn[tb % 2], hnb[tb % 2]
            for dc in range(8):
                bg, be = 2 + (dc % 2), 4 + (dc % 2)
                tht, thbuf, t2t, t2buf = th[dc % 2], thb[dc % 2], t2[dc % 2], t2b[dc % 2]
                for kc in range(8):
                    S.op("pe", lambda e, dc=dc, kc=kc, bg=bg: e.matmul(
                        banks[bg], wg[:, kc, dc * 128:(dc + 1) * 128], hnt[:, kc, :], start=(kc == 0), stop=(kc == 7)),
                        reads=[wgb, hnbuf], writes=[bankb[bg]])
                for kc in range(2):
                    S.op("pe", lambda e, dc=dc, kc=kc, be=be: e.matmul(
                        banks[be], wp[:, kc, dc * 128:(dc + 1) * 128], pt[:, kc, sl], start=(kc == 0), stop=(kc == 1)),
                        reads=[wpb, ptb], writes=[bankb[be]])
                S.op("act", lambda e, bg=bg, tht=tht: e.activation(out=tht, in_=banks[bg], func=AF.Tanh, scale=0.5),
                     reads=[bankb[bg]], writes=[thbuf])
                S.op("dve", lambda e, be=be, tht=tht, t2t=t2t: e.scalar_tensor_tensor(
                    out=t2t, in0=tht, scalar=1.0, in1=banks[be], op0=ALU.add, op1=ALU.mult),
                    reads=[bankb[be], thbuf], writes=[t2buf])
                S.op("dve", lambda e, dc=dc, t2t=t2t: e.scalar_tensor_tensor(
                    out=h[:, dc, sl], in0=t2t, scalar=0.5, in1=h[:, dc, sl], op0=ALU.mult, op1=ALU.add),
                    reads=[t2buf], writes=[hb[tb]])

        def stD(tb):
            sl = slice(tb * BLK, (tb + 1) * BLK)
            gcols = gs[:, 32:40] if last else gs[:, 8 * (l + 1):8 * (l + 2)]
            xt, xb = xo[tb % len(xo)], xob[tb % len(xo)]
            self.norm_block(h[:, :, sl], hb[tb], gcols, xt, xb, tmps[1])
            if last:
                S.dma("sp", f"xo{tb % len(xo)}", lambda e: e.dma_start(out=ov[:, :, sl], in_=xt), reads=[xb])
            else:
                self.emit_xn_out(l + 1, tb, xt, xb)
                self.emit_h_out(tb, h, hb[tb])

        order = ["A0", "A1", "B0", "A2", "B1", "C0", "A3", "B2", "D0", "C1", "B3", "D1", "C2", "D2", "C3", "D3"]
        fmap = {"A": stA, "B": stB, "C": stC, "D": stD}
        for o in order:
            fmap[o[0]](int(o[1]))

    def phase_A(self, l, w_in, ropeC, ropeS, lamv):
        S = self.S
        c = self.ctx
        ar = c["ar"]
        banks, bankb = c["banks"], c["bankb"]
        ones, swapm, MA, MB = c["ones"], c["swapm"], c["MA"], c["MB"]
        cbuf = c["cbuf"]
        bd = self.bdeps()
        gs = c["gains_sb"]
        lambda_init = 0.8 - 0.6 * math.exp(-0.3 * l)

        if self.fused:
            xn_full = None
            y_out = None
        else:
            xn_full = self.din("xn_full", [4 * D, TQ], BF16)
            y_out = self.dout("y_part", [256, SEQ], BF16)

        w = ar.alloc([8, 1024], BF16)
        wb = Buf()
        xnt = [ar.alloc([8, BLK], BF16) for _ in range(2)]
        xnb = [Buf(), Buf()]
        KAT = ar.alloc([SEQ], BF16)
        KBT = ar.alloc([SEQ], BF16)
        VA = ar.alloc([32, 192], BF16)
        V4 = ar.alloc([2, 16, 192], BF16)
        V16 = ar.alloc([2, 16, 192], BF16)
        v4b = [Buf(), Buf()]
        v16b = [Buf(), Buf()]
        vscr = self.dint(f"vscr{l}", [SEQ, 128], BF16)
        vsb = [Buf() for _ in range(NBLK)]
        VB = ar.alloc([64, 128], BF16)
        kab = [Buf() for _ in range(NBLK)]
        kbb = [Buf() for _ in range(NBLK)]
        vab = [Buf() for _ in range(NBLK)]
        vbb = [Buf() for _ in range(NBLK)]
        rC = [ar.alloc([BLK], F32) for _ in range(2)]
        rS = [ar.alloc([BLK], F32) for _ in range(2)]
        rb = [Buf(), Buf()]
        QS = [ar.alloc([TQ], BF16) for _ in range(2)]
        qsb = [[Buf() for _ in range(4)] for _ in range(2)]
        qB = ar.alloc([BLK], BF16)
        qbb = Buf()
        GS = [ar.alloc([TQ], BF16) for _ in range(2)]
        gsb = [[Buf() for _ in range(4)] for _ in range(2)]
        gBs = [ar.alloc([BLK], BF16) for _ in range(2)]
        gbbs = [Buf(), Buf()]
        da = ar.alloc([BLK], F32)
        db = ar.alloc([BLK], F32)
        dcc = ar.alloc([BLK], F32)
        dab, dbb, dcb = Buf(), Buf(), Buf()
        qraw = ar.alloc([BLK], BF16)
        qrb = Buf()
        t1 = ar.alloc([BLK], F32)
        t2 = ar.alloc([BLK], F32)
        t1b, t2b = Buf(), Buf()
        NP = 8
        PP = [ar.alloc([2, BLK], BF16) for _ in range(4)]
        P = [PP[k // 2][:, k % 2, :] for k in range(NP)]
        Pb = [Buf() for _ in range(NP)]
        psall = c["psall"]
        PS = [ar.alloc([2, BLK], BF16) for _ in range(2)]
        psb = [Buf(), Buf()]
        ta, tb_ = da, db
        tc = ar.alloc([BLK], F32)
        tab, tbb, tcb = dab, dbb, Buf()
        sqt = ar.alloc([BLK], BF16)
        sqb = Buf()
        YA = ar.alloc([TQ], BF16)
        yasb = Buf()
        yB = [ar.alloc([BLK], BF16) for _ in range(2)]
        ybb = [Buf(), Buf()]
        lam_t = ar.alloc([4 * 64], F32)
        lam_s = ar.alloc([8], F32)
        lamb = Buf()

        if self.fused and self.wpre.get(l) is not None:
            wb = self.wpre[l]
        else:
            S.dma("pool", "w", lambda e: e.dma_start(out=w, in_=w_in[l].rearrange("(k p) n -> p k n", p=128)),
                  writes=[wb], extra=bd)
        S.dma("sp", "lam", lambda e: e.dma_start(out=lam_t, in_=lamv[:, l * 256:(l + 1) * 256]),
              writes=[lamb], extra=bd)
        S.op("pool", lambda e: e.memset(VA[:, :, 64:128], 1.0), writes=vab, extra=bd)
        S.op("pool", lambda e: e.memset(V4[:, :, :, 64:128], 1.0), writes=v4b, extra=bd)
        S.op("pool", lambda e: e.memset(V16[:, :, :, 64:128], 1.0), writes=v16b, extra=bd)
        S.op("dve", lambda e: e.tensor_tensor(out=lam_t[:, 0:64], in0=lam_t[:, 0:64], in1=lam_t[:, 64:128], op=ALU.mult),
             reads=[lamb], writes=[lamb], extra=bd)
        S.op("dve", lambda e: e.tensor_tensor(out=lam_t[:, 128:192], in0=lam_t[:, 128:192], in1=lam_t[:, 192:256], op=ALU.mult),
             reads=[lamb], writes=[lamb])
        S.op("dve", lambda e: e.reduce_sum(out=lam_s[:, 0:1], in_=lam_t[:, 0:64], axis=mybir.AxisListType.X),
             reads=[lamb], writes=[lamb])
        S.op("dve", lambda e: e.reduce_sum(out=lam_s[:, 1:2], in_=lam_t[:, 128:192], axis=mybir.AxisListType.X),
             reads=[lamb], writes=[lamb])
        S.op("act", lambda e: e.activation(out=lam_s[:, 2:4], in_=lam_s[:, 0:2], func=AF.Exp),
             reads=[lamb], writes=[lamb])
        S.op("dve", lambda e: e.tensor_tensor(out=lam_s[:, 4:5], in0=lam_s[:, 3:4], in1=lam_s[:, 2:3], op=ALU.subtract),
             reads=[lamb], writes=[lamb])
        S.op("dve", lambda e: e.tensor_scalar(out=lam_s[:, 4:5], in0=lam_s[:, 4:5], scalar1=-lambda_init, scalar2=None,
                                              op0=ALU.add), reads=[lamb], writes=[lamb])
        S.op("dve", lambda e: e.tensor_scalar(out=lam_s[:, 5:6], in0=gs[:, 40 + l:41 + l], scalar1=(1.0 - lambda_init),
                                              scalar2=None, op0=ALU.mult), reads=[lamb, c["gbuf"]], writes=[lamb])
        neglam = lam_s[:, 4:5]
        sgain = lam_s[:, 5:6]
        eps_sub = c["eps_t"][:, 1:2]

        if self.fused:
            xgv = [g.rearrange("(r k p) t -> p r k t", p=128, k=8) for g in self.xg[l]]
        else:
            xv = xn_full.rearrange("(r k p) t -> p r k t", p=128, k=8)
            yov = y_out
        pi = [0]

        def nextP():
            i = pi[0] % NP
            pi[0] += 1
            return P[i], Pb[i]

        qraws = [qraw, ar.alloc([BLK], BF16)]
        qrbs = [qrb, Buf()]
        t1s = [t1, ar.alloc([BLK], F32)]
        t2s = [t2, ar.alloc([BLK], F32)]
        t1bs = [t1b, Buf()]
        t2bs = [t2b, Buf()]

        def rope_cast(bank_i, u):
            S.op("dve", lambda e: e.tensor_copy(out=qraws[u], in_=banks[bank_i]), reads=[bankb[bank_i]], writes=[qrbs[u]])

        def rope_rest(bank_i, dst, dstbuf, rbi, u):
            bq, bqb = banks[bank_i], bankb[bank_i]
            S.op("pe", lambda e: e.matmul(banks[2], swapm, qraws[u], start=True, stop=True),
                 reads=[qrbs[u], cbuf], writes=[bankb[2]])
            S.op("dve", lambda e: e.tensor_tensor(out=t1s[u], in0=bq, in1=rC[rbi], op=ALU.mult),
                 reads=[bqb, rb[rbi]], writes=[t1bs[u]])
            S.op("dve", lambda e: e.tensor_tensor(out=t2s[u], in0=banks[2], in1=rS[rbi], op=ALU.mult),
                 reads=[bankb[2], rb[rbi]], writes=[t2bs[u]])
            S.op("pool", lambda e: e.tensor_tensor(out=dst, in0=t1s[u], in1=t2s[u], op=ALU.add),
                 reads=[t1bs[u], t2bs[u]], writes=[dstbuf])

        def gate_evac(bank_i, dst, dstbuf):
            bq, bqb = banks[bank_i], bankb[bank_i]
            S.op("act", lambda e: e.activation(out=t1, in_=bq, func=AF.Tanh, scale=0.5), reads=[bqb], writes=[t1b])
            S.op("pool", lambda e: e.tensor_scalar(out=t1, in0=t1, scalar1=0.5, scalar2=0.5, op0=ALU.mult, op1=ALU.add),
                 reads=[t1b], writes=[t1b])
            S.op("dve", lambda e: e.tensor_tensor(out=dst, in0=bq, in1=t1, op=ALU.mult),
                 reads=[bqb, t1b], writes=[dstbuf])

        def emit_loads(i):
            xt, xb = xnt[i % 2], xnb[i % 2]
            rbi = i % 2
            sl = slice(i * BLK, (i + 1) * BLK)
            if self.fused:
                S.dma("sp", f"xn{i % 2}", lambda e: e.dma_start(out=xt, in_=xgv[i // 4][:, i % 4, :, :]),
                      reads=[self.xgb[l][i // 4]], writes=[xb], extra=bd)
            else:
                S.dma("sp", f"xn{i % 2}", lambda e: e.dma_start(
                    out=xt, in_=xv[:, i // 4, :, (i % 4) * BLK:(i % 4 + 1) * BLK]), writes=[xb], extra=bd)
            S.dma("sp", f"rc{rbi}", lambda e: e.dma_start(out=rC[rbi], in_=ropeC[:, sl]), writes=[rb[rbi]], extra=bd)
            S.dma("sp", f"rs{rbi}", lambda e: e.dma_start(out=rS[rbi], in_=ropeS[:, sl]), writes=[rb[rbi]], extra=bd)

        def diff_epilogue(i):
            sl = slice(i * BLK, (i + 1) * BLK)
            sbi, qs = i // 4, slice((i % 4) * BLK, (i % 4 + 1) * BLK)
            O0, O1, D0, D1 = banks[4], banks[5], banks[6], banks[7]
            gB, gbb = gBs[i % 2], gbbs[i % 2]
            yb_, ybbuf = yB[i % 2], ybb[i % 2]
            Dz = banks[6]
            S.op("act", lambda e: e.activation(out=dcc, in_=Dz, func=AF.Ln), reads=[bankb[6]], writes=[dcb])
            S.op("act", lambda e: e.activation(out=dcc, in_=dcc, func=AF.Exp, scale=-1.0), reads=[dcb], writes=[dcb])
            for hf in range(2):
                rr = slice(64 * hf, 64 * hf + 64)
                S.op("dve", lambda e, rr=rr: e.tensor_tensor(out=da[rr, :], in0=O0[rr, :], in1=dcc[0:64, :], op=ALU.mult),
                     reads=[bankb[4], dcb], writes=[dab])
                S.op("dve", lambda e, rr=rr: e.tensor_tensor(out=db[rr, :], in0=O1[rr, :], in1=dcc[64:128, :], op=ALU.mult),
                     reads=[bankb[5], dcb], writes=[dbb])
            S.op("dve", lambda e: e.scalar_tensor_tensor(out=da, in0=db, scalar=neglam, in1=da, op0=ALU.mult, op1=ALU.add),
                 reads=[dbb, dab, lamb], writes=[dab])
            S.op("act", lambda e: e.activation(out=sqt, in_=da, func=AF.Square), reads=[dab], writes=[sqb])

        def diff_epilogue2(i):
            sl = slice(i * BLK, (i + 1) * BLK)
            sbi, qs = i // 4, slice((i % 4) * BLK, (i % 4 + 1) * BLK)
            gB, gbb = gBs[i % 2], gbbs[i % 2]
            yb_, ybbuf = yB[i % 2], ybb[i % 2]
            S.op("pe", lambda e: e.matmul(banks[7], ones, sqt, start=True, stop=True), reads=[sqb, cbuf], writes=[bankb[7]])
            S.op("act", lambda e: e.activation(out=dcc, in_=banks[7], func=AF.Ln, scale=1.0 / 128, bias=eps_sub),
                 reads=[bankb[7], c["gbuf"]], writes=[dcb])
            S.op("act", lambda e: e.activation(out=dcc, in_=dcc, func=AF.Exp, scale=-0.5), reads=[dcb], writes=[dcb])
            S.op("dve", lambda e: e.scalar_tensor_tensor(out=da, in0=da, scalar=sgain, in1=dcc, op0=ALU.mult, op1=ALU.mult),
                 reads=[dab, dcb, lamb], writes=[dab])
            S.op("pool", lambda e: e.tensor_tensor(out=yb_, in0=da, in1=gB, op=ALU.mult),
                 reads=[dab, gbb], writes=[ybbuf])
            if self.fused:
                S.dma("sp", f"yb{i % 2}", lambda e: e.dma_start(out=self.yp[l][sbi][128:256, qs], in_=yb_),
                      reads=[ybbuf], writes=[self.ypb[l][sbi]])
            else:
                S.dma("sp", f"yb{i % 2}", lambda e: e.dma_start(out=yov[128:256, sl], in_=yb_), reads=[ybbuf])

        LT4, UT4 = MA[:, 0:512], MA[:, 512:1024]
        M1 = [MA[:, 1024 + 512 * r:1536 + 512 * r] for r in range(4)]

        def dswa_superblock(n):
            par = n % 2
            t0 = TQ * n
            Q, G = QS[par], GS[par]
            qb_all = qsb[par]

            def class_batches(h, r4):
                hr = slice(64 * h, 64 * h + 64)
                vs = slice(64 * h, 64 * h + 128)
                bl = []
                for half in range(2):
                    tiles = []
                    mms = list(range(8)) if half == 0 else list(range(8, 15))
                    for t_, mm in enumerate(mms):
                        m = 16 * n + mm
                        tiles.append(dict(sc=slice(64 * t_, 64 * t_ + 64), k=KAT[hr, 128 * m:128 * m + 128],
                                          q=Q[hr, 128 * mm + r4:128 * mm + 256:4], kb=[kab[m // 4]],
                                          oc=slice(32 * mm, 32 * mm + 64), v=VA[:, m % 32, vs], vb=[vab[m // 4]]))
                    if half == 1:
                        m = 16 * n + 15
                        tiles.append(dict(sc=slice(448, 480), k=KAT[hr, 128 * m:128 * m + 128],
                                          q=Q[hr, 1920 + r4:2048:4], kb=[kab[m // 4]],
                                          oc=slice(480, 512), v=VA[:, m % 32, vs], vb=[vab[m // 4]]))
                        if n > 0:
                            m = 16 * n - 1
                            tiles.append(dict(sc=slice(480, 512), k=KAT[hr, 128 * m:128 * m + 128],
                                              q=Q[hr, r4:128:4], kb=[kab[m // 4]],
                                              oc=slice(0, 32), v=VA[:, m % 32, vs], vb=[vab[m // 4]]))
                    bl.append((M1[r4], tiles))
                for prev in range(2):
                    tiles = []
                    for j in range(4):
                        jj, nn = (j, n) if not prev else ((j - 1, n) if j > 0 else (3, n - 1))
                        if nn < 0:
                            continue
                        ks = TQ * nn + BLK * jj + r4
                        tiles.append(dict(sc=slice(128 * j, 128 * j + 128), k=KAT[hr, ks:ks + 4 * 127 + 1:4],
                                          q=Q[hr, BLK * j + r4:BLK * (j + 1):4], kb=[kab[4 * nn + jj]],
                                          oc=slice(128 * j, 128 * j + 128), v=V4[:, nn % 2, 4 * r4 + jj, vs],
                                          vb=[v4b[nn % 2]]))
                    bl.append((UT4 if prev else LT4, tiles))
                for prev in range(2):
                    nn = n - prev
                    if nn < 0:
                        continue
                    tiles = []
                    for s_ in range(4):
                        r16 = r4 + 4 * s_
                        ks = TQ * nn + r16
                        tiles.append(dict(sc=slice(128 * s_, 128 * s_ + 128), k=KAT[hr, ks:ks + 16 * 127 + 1:16],
                                          q=Q[hr, r16:TQ:16], kb=[kab[4 * nn + q_] for q_ in range(4)],
                                          oc=slice(s_, 512, 4), v=V16[:, nn % 2, r16, vs], vb=[v16b[nn % 2]]))
                    bl.append((UT4 if prev else LT4, tiles))
                return bl

            batches = []
            for r4 in range(4):
                b0, b1 = class_batches(0, r4), class_batches(1, r4)
                for bi in range(len(b0)):
                    batches.append(dict(mask=b0[bi][0], tiles=(b0[bi][1], b1[bi][1]), accs=(4 + 2 * (r4 % 2), 5 + 2 * (r4 % 2)),
                                        first=(bi == 0), last=(bi == len(b0) - 1), r4=r4))
            nb_ = len(batches)

            def e_S(b_):
                B = batches[b_]
                pr = b_ % 2
                for T0_, T1_ in zip(*B["tiles"]):
                    for hh_, T in ((0, T0_), (1, T1_)):
                        S.op("pe", lambda e, T=T, hh_=hh_: e.matmul(banks[2 * pr + hh_][:, T["sc"]], T["k"], T["q"],
                                                                 start=True, stop=True),
                             reads=T["kb"] + qb_all, writes=[bankb[2 * pr + hh_]])

            def e_exp(b_):
                B = batches[b_]
                pr = b_ % 2
                S.op("act", lambda e: e.activation(out=PP[pr], in_=psall[:, 2 * pr:2 * pr + 2, :], func=AF.Exp, scale=0.125),
                     reads=[bankb[2 * pr], bankb[2 * pr + 1]], writes=[Pb[2 * pr], Pb[2 * pr + 1]])
                for hh_ in range(2):
                    S.op("dve", lambda e, hh_=hh_: e.tensor_tensor(out=PP[pr][:, hh_, :], in0=PP[pr][:, hh_, :], in1=B["mask"],
                                                                 op=ALU.mult), reads=[Pb[2 * pr + hh_], cbuf], writes=[Pb[2 * pr + hh_]])

            def e_PV(b_):
                B = batches[b_]
                pr = b_ % 2
                nt = len(B["tiles"][0])
                if B["first"]:
                    for hh_ in range(2):
                        acc, accb = banks[B["accs"][hh_]], bankb[B["accs"][hh_]]
                        S.op("pe", lambda e, acc=acc: e.matmul(acc, MB[:, 0:128], MA[:, 0:512], start=True, stop=False),
                             reads=[cbuf], writes=[accb])
                for ti, (T0_, T1_) in enumerate(zip(*B["tiles"])):
                    for hh_, T in ((0, T0_), (1, T1_)):
                        acc, accb = banks[B["accs"][hh_]], bankb[B["accs"][hh_]]
                        S.op("pe", lambda e, T=T, ti=ti, hh_=hh_, acc=acc: e.matmul(
                            acc[:, T["oc"]], T["v"], PP[pr][:, hh_, T["sc"]], start=False,
                            stop=(B["last"] and ti == nt - 1)), reads=T["vb"] + [Pb[2 * pr + hh_]], writes=[accb])
                if B["last"]:
                    r4 = B["r4"]
                    for h in range(2):
                        acc, accb = banks[B["accs"][h]], bankb[B["accs"][h]]
                        tA, tAb, tB, tBb = (ta, tab, tb_, tbb) if h == 0 else (tc, tcb, dcc, dcb)
                        ro = slice(64 * h, 64 * h + 64)
                        rd = slice(64 * (1 - h), 64 * (1 - h) + 64)
                        S.op("act", lambda e, acc=acc, rd=rd, tA=tA: e.activation(out=tA[rd, :], in_=acc[rd, :], func=AF.Ln),
                             reads=[accb], writes=[tAb])
                        S.op("act", lambda e, rd=rd, tA=tA: e.activation(out=tA[rd, :], in_=tA[rd, :], func=AF.Exp, scale=-1.0),
                             reads=[tAb], writes=[tAb])
                        S.op("dve", lambda e, acc=acc, ro=ro, rd=rd, tA=tA, tB=tB: e.tensor_tensor(
                            out=tB[ro, :], in0=acc[ro, :], in1=tA[rd, :], op=ALU.mult), reads=[accb, tAb], writes=[tBb])
                        S.op("pool", lambda e, ro=ro, tB=tB: e.tensor_tensor(out=YA[ro, r4:TQ:4], in0=tB[ro, :], in1=G[ro, r4:TQ:4],
                                                                           op=ALU.mult), reads=[tBb] + gsb[par], writes=[yasb])

            e_S(0)
            for b_ in range(nb_):
                if b_ + 1 < nb_:
                    e_S(b_ + 1)
                e_exp(b_)
                e_PV(b_)
            if self.fused:
                S.dma("sp", "ya", lambda e: e.dma_start(out=self.yp[l][n][0:128, :], in_=YA),
                      reads=[yasb], writes=[self.ypb[l][n]])
                S.cc(f"yg{l}_{n}", lambda e: e.collective_compute(
                    "AllGather", ALU.bypass, replica_groups=self.GROUPS, ins=[self.yp[l][n]],
                    outs=[self.yf[l][8 * n:8 * n + 8].rearrange("k p t -> (k p) t")]),
                    reads=[self.ypb[l][n]], writes=[self.yfb[l][n]])
            else:
                S.dma("sp", "ya", lambda e: e.dma_start(out=yov[0:128, t0:t0 + TQ], in_=YA), reads=[yasb])

        emit_loads(0)
        for i in range(NBLK):
            xt, xb = xnt[i % 2], xnb[i % 2]
            rbi = i % 2
            sl = slice(i * BLK, (i + 1) * BLK)
            if self.fused and i in (1, 5, 9):
                self.def_cc[l][(i + 3) // 4]()
            if self.fused and i == 10:
                wo_t, _ = ar.alloc_at(WO_OFF, [8, D], BF16)
                self.wo_pre[l] = Buf()
                S.dma("pool", "wo", lambda e: e.dma_start(out=wo_t, in_=self.w_out_ap[l].rearrange("(k p) n -> p k n", p=128)),
                      writes=[self.wo_pre[l]], extra=bd)
            sq_ = slice((i % 4) * BLK, (i % 4 + 1) * BLK)
            spar = (i // 4) % 2

            def proj_group(gc, bk):
                for kc in range(8):
                    S.op("pe", lambda e, kc=kc, xt=xt: e.matmul(
                        banks[bk], w[:, kc, gc * 128:(gc + 1) * 128], xt[:, kc, :], start=(kc == 0), stop=(kc == 7)),
                        reads=[wb, xb], writes=[bankb[bk]])

            def v_group(t, bk):
                kt = 4 * i + t
                for kc in range(8):
                    S.op("pe", lambda e, kc=kc, xt=xt: e.matmul(
                        banks[bk][:, 0:256], xt[:, kc, t * 128:(t + 1) * 128], w[:, kc, 768:1024],
                        start=(kc == 0), stop=(kc == 7)), reads=[wb, xb], writes=[bankb[bk]])
                S.op("dve", lambda e: e.tensor_copy(out=VA[:, kt % 32, 0:64], in_=banks[bk][:, 0:64]),
                     reads=[bankb[bk]], writes=[vab[i]])
                S.op("dve", lambda e: e.tensor_copy(out=VA[:, kt % 32, 128:192], in_=banks[bk][:, 64:128]),
                     reads=[bankb[bk]], writes=[vab[i]])
                S.op("dve", lambda e: e.tensor_copy(out=VB[:, kt, :], in_=banks[bk][:, 128:256]),
                     reads=[bankb[bk]], writes=[vbb[i]])

            ropes = [(0, 0, QS[spar][:, sq_], qsb[spar][i % 4]), (1, 1, KAT[:, sl], kab[i]),
                     (3, 3, qB, qbb), (4, 0, KBT[:, sl], kbb[i])]
            for u_, (gc, bk, dst, dstb) in enumerate(ropes):
                proj_group(gc, bk)
                rope_cast(bk, u_ % 2)
                if u_ > 0:
                    pg, pb_, pd, pdb = ropes[u_ - 1]
                    rope_rest(pb_, pd, pdb, rbi, (u_ - 1) % 2)
            if i > 0:
                diff_epilogue(i - 1)
            v_group(0, 1)
            pg, pb_, pd, pdb = ropes[3]
            rope_rest(pb_, pd, pdb, rbi, 1)
            v_group(1, 3)
            v_group(2, 0)
            v_group(3, 1)
            if i > 0:
                diff_epilogue2(i - 1)
            proj_group(2, 3)
            gate_evac(3, GS[spar][:, sq_], gsb[spar][i % 4])
            proj_group(5, 0)
            gate_evac(0, gBs[i % 2], gbbs[i % 2])

            s0 = (4 * i) % 32
            for hh_ in range(2):
                S.dma("sp", f"vo{hh_}", lambda e, s0=s0, i=i, hh_=hh_: e.dma_start(
                    out=vscr[BLK * i:BLK * (i + 1), 64 * hh_:64 * hh_ + 64].rearrange("(t p) c -> p t c", p=128),
                    in_=VA[:, s0:s0 + 4, 128 * hh_:128 * hh_ + 64]), reads=[vab[i]], writes=[vsb[i]])
            if i % 4 == 3:
                n_sb = i // 4
                vpar = n_sb % 2
                for hh_ in range(2):
                    src16 = vscr[TQ * n_sb:TQ * (n_sb + 1), 64 * hh_:64 * hh_ + 64].rearrange("(p r) c -> p r c", r=16)
                    S.dma("sp", f"v16_{vpar}", lambda e, vpar=vpar, src16=src16, hh_=hh_: e.dma_start(
                        out=V16[:, vpar, :, 128 * hh_:128 * hh_ + 64], in_=src16),
                        reads=[vsb[4 * n_sb + q_] for q_ in range(4)], writes=[v16b[vpar]])
                    for j in range(4):
                        r0 = TQ * n_sb + BLK * j
                        src4 = vscr[r0:r0 + BLK, 64 * hh_:64 * hh_ + 64].rearrange("(p r) c -> p r c", r=4)
                        S.dma("sp", f"v4_{vpar}", lambda e, vpar=vpar, src4=src4, j=j, hh_=hh_: e.dma_start(
                            out=V4[:, vpar, j::4, 128 * hh_:128 * hh_ + 64], in_=src4),
                            reads=[vsb[4 * n_sb + j]], writes=[v4b[vpar]])

            if i + 1 < NBLK:
                emit_loads(i + 1)

            if i % 4 == 0 and i > 0:
                dswa_superblock(i // 4 - 1)

            O0, O1, D0, D1 = banks[4], banks[5], banks[6], banks[7]
            nk = 4 * i + 4

            def d_geo(kt):
                j = kt - 4 * i
                c0 = 128 * max(0, j)
                return j, c0, slice(c0, BLK)

            def d_S(kt):
                par = kt % 2
                j, c0, cs = d_geo(kt)
                ksl = slice(kt * 128, (kt + 1) * 128)
                for comp in range(2):
                    rs_ = slice(64 * comp, 64 * comp + 64)
                    S.op("pe", lambda e, rs_=rs_, comp=comp: e.matmul(
                        banks[2 * par + comp][:, cs], KBT[rs_, ksl], qB[rs_, cs], start=True, stop=True),
                        reads=[kbb[kt // 4], qbb], writes=[bankb[2 * par + comp]])

            def d_exp(kt):
                par = kt % 2
                pq = kt % 4
                j, c0, cs = d_geo(kt)
                S.op("act", lambda e: e.activation(out=PP[pq][:, :, cs], in_=psall[:, 2 * par:2 * par + 2, cs],
                                                   func=AF.Exp, scale=0.125),
                     reads=[bankb[2 * par], bankb[2 * par + 1]], writes=[Pb[2 * pq], Pb[2 * pq + 1]])
                if j >= 0:
                    for comp in range(2):
                        S.op("dve", lambda e, comp=comp: e.tensor_tensor(
                            out=PP[pq][:, comp, c0:c0 + 128], in0=PP[pq][:, comp, c0:c0 + 128], in1=MB[:, 384:512],
                            op=ALU.mult), reads=[Pb[2 * pq + comp], cbuf], writes=[Pb[2 * pq + comp]])

            def d_PV(kt):
                par = kt % 4
                j, c0, cs = d_geo(kt)
                for comp in range(2):
                    Pt, Ptb = P[2 * par + comp], Pb[2 * par + comp]
                    S.op("pe", lambda e, comp=comp, Pt=Pt, nk=nk: e.matmul(
                        banks[4 + comp][:, cs], VB[:, kt, :], Pt[:, cs], start=(kt == 0), stop=(kt == nk - 1)),
                        reads=[vbb[kt // 4], Ptb], writes=[bankb[4 + comp]])

            def d_D(kt):
                par = kt % 2
                pq = kt % 4
                j, c0, cs = d_geo(kt)
                if j >= 0:
                    for comp in range(2):
                        Pt, Ptb = P[2 * pq + comp], Pb[2 * pq + comp]
                        S.op("pe", lambda e, comp=comp, Pt=Pt, nk=nk: e.matmul(
                            banks[6][64 * comp:64 * comp + 64, cs], ones[:, 0:64], Pt[:, cs],
                            start=(kt == 0), stop=(kt == nk - 1), tile_position=(0, 64 * comp)),
                            reads=[cbuf, Ptb], writes=[bankb[6]])
                elif par == 1:
                    q_ = (kt // 2) % 2
                    pa_, pb2 = (kt - 1) % 4, kt % 4
                    S.op("dve", lambda e: e.tensor_tensor(out=PS[q_], in0=PP[pa_], in1=PP[pb2], op=ALU.add),
                         reads=[Pb[2 * pa_], Pb[2 * pa_ + 1], Pb[2 * pb2], Pb[2 * pb2 + 1]], writes=[psb[q_]])

            def d_Dpair(kt):
                q_ = (kt // 2) % 2
                for comp in range(2):
                    S.op("pe", lambda e, comp=comp: e.matmul(
                        banks[6][64 * comp:64 * comp + 64, :], ones[:, 0:64], PS[q_][:, comp, :],
                        start=(kt == 1), stop=False, tile_position=(0, 64 * comp)),
                        reads=[cbuf, psb[q_]], writes=[bankb[6]])

            d_S(0)
            for kt in range(nk):
                if kt + 1 < nk:
                    d_S(kt + 1)
                d_exp(kt)
                d_PV(kt)
                if kt >= 2 and (kt - 1) % 2 == 1 and (kt - 1) < 4 * i:
                    d_Dpair(kt - 1)
                d_D(kt)

        diff_epilogue(NBLK - 1)
        diff_epilogue2(NBLK - 1)
        dswa_superblock(NBLK // 4 - 1)


def _tok_idx(j):
    return np.concatenate([np.arange(BLK * (4 * s_ + j), BLK * (4 * s_ + j + 1)) for s_ in range(4)])


def _rope_tables():
    half = 8
    inv = np.power(np.float32(ROPE_THETA), -np.arange(half, dtype=np.float32) * np.float32(2.0 / 16)).astype(np.float32)
    pos = np.arange(SEQ, dtype=np.float32)
    ang = pos[None, :] * inv[:, None]
    cos, sin = np.cos(ang).astype(np.float32), np.sin(ang).astype(np.float32)
    C = np.ones((128, SEQ), np.float32)
    Sg = np.zeros((128, SEQ), np.float32)
    for hb in (0, 64):
        C[hb:hb + 8] = cos
        C[hb + 8:hb + 16] = cos
        Sg[hb:hb + 8] = -sin
        Sg[hb + 8:hb + 16] = sin
    return C, Sg


def _consts_bf():
    ones = np.ones((128, 128), np.float32)
    sw = np.zeros((128, 128), np.float32)
    for m in range(128):
        mm = m % 64
        if mm < 8:
            sw[m + 8, m] = 1.0
        elif mm < 16:
            sw[m - 8, m] = 1.0
    p = np.arange(128)[:, None]
    i_ = np.arange(128)[None, :]
    LT = (i_ >= p).astype(np.float32)
    UT = (p >= i_).astype(np.float32)
    c_ = np.arange(64)[None, :]
    M1 = [((4 * c_ + r - p >= 0) & (4 * c_ + r - p <= 128)).astype(np.float32) for r in range(4)]
    cA = np.concatenate([np.tile(LT, (1, 4)), np.tile(UT, (1, 4))] + [np.tile(m, (1, 8)) for m in M1], axis=1)
    z2 = np.arange(MB_W)[None, :] - 384
    cB = ((z2 - p) >= 0).astype(np.float32)
    return np.concatenate([ones, sw, cA, cB], axis=1).astype(BF)


_PROGS = {}


def _prog(phases, fused):
    key = (tuple(phases), fused)
    if key not in _PROGS:
        b = Builder(list(phases), fused)
        _PROGS[key] = b.build()
    return _PROGS[key]


def _host_prep(x, p, attn_norm_gain, w_in, w_out, lambda_q1, lambda_k1, lambda_q2, lambda_k2, subln_gain,
               ple_norm_gain, w_ple_gate, w_ple, final_norm_gain):
    f = lambda a: np.ascontiguousarray(np.asarray(a, dtype=np.float32))
    x, p = f(x), f(p)
    w_in, w_out, w_ple_gate, w_ple = f(w_in), f(w_out), f(w_ple_gate), f(w_ple)
    C, Sg = _rope_tables()
    cb = _consts_bf()

    def pcol(v):
        return np.asarray(v, np.float32).reshape(8, 128).T

    gains = np.zeros((128, 48), np.float32)
    gains[:, 0:8] = pcol(attn_norm_gain[0])
    gains[:, 8:16] = pcol(attn_norm_gain[1])
    gains[:, 16:24] = pcol(ple_norm_gain[0])
    gains[:, 24:32] = pcol(ple_norm_gain[1])
    gains[:, 32:40] = pcol(final_norm_gain)
    gains[:, 40] = np.asarray(subln_gain[0], np.float32)
    gains[:, 41] = np.asarray(subln_gain[1], np.float32)
    lamv = np.zeros((128, DEPTH * 256), np.float32)
    for l in range(DEPTH):
        for k, v in enumerate((lambda_q1, lambda_k1, lambda_q2, lambda_k2)):
            lamv[:, l * 256 + k * 64:l * 256 + (k + 1) * 64] = np.asarray(v[l], np.float32)[None, :]
    per_core = []
    for c in range(8):
        b, j = c // 4, c % 4
        cols = np.concatenate([np.arange(base + 128 * j, base + 128 * j + 128)
                               for base in (0, 512, 1536, 2048, 2560, 3584, 1024, 3072)])
        rows = np.concatenate([np.concatenate([np.arange(128 * r, 128 * r + 128), np.arange(512 + 128 * r, 512 + 128 * r + 128)])
                               for r in range(4)])
        d = dict(
            gains=gains, consts_bf=cb, ropeC=C, ropeS=Sg, lamv=lamv,
            xT=np.ascontiguousarray(x[b, _tok_idx(j), :].T),
            pT=np.ascontiguousarray(np.transpose(p[:, b, _tok_idx(j), :], (0, 2, 1))),
            w_in=np.ascontiguousarray(w_in[:, :, cols]),
            w_out=np.ascontiguousarray(w_out[:, rows, :]),
            w_gate=w_ple_gate, w_ple=w_ple,
        )
        per_core.append(d)
    return per_core


def _run(phases, fused, in_maps):
    nc = _prog(phases, fused)
    res = run_bass_kernel_spmd(nc, in_maps, core_ids=list(range(8)))
    return res.results


def _sel(d, names):
    return {k: d[k] for k in names}


def kernel_fused(**inputs):
    pc = _host_prep(**inputs)
    names = ["gains", "consts_bf", "xT", "w_in", "ropeC", "ropeS", "lamv", "w_out", "w_gate", "w_ple", "pT"]
    r = _run(["T0", "A1", "T1", "A2", "T2"], True, [_sel(pc[c], names) for c in range(8)])
    out = np.zeros((NB, SEQ, D), np.float32)
    for c in range(8):
        out[c // 4, _tok_idx(c % 4), :] = r[c]["outT"].T
    return out


def kernel(**inputs):
    return kernel_fused(**inputs)
```

```python
import math
import numpy as np
import ml_dtypes
import concourse.bass as bass
import concourse.mybir as mybir
from concourse.bass_utils import run_bass_kernel_spmd

F32 = mybir.dt.float32
BF16 = mybir.dt.bfloat16
AF = mybir.ActivationFunctionType
ALU = mybir.AluOpType
BF = ml_dtypes.bfloat16

D = 1024
SEQ = 8192
NB = 2
DEPTH = 2
TQ = 2048
BLK = 512
NBLK = SEQ // BLK
RMS_EPS = 1e-6
SUBLN_EPS = 1e-5
ROPE_THETA = 500000.0
NDELTA_A = 17
MA_W = 6 * 512
MB_W = 384 + 512

WO_OFF = 190 * 1024
ENGS = ("pe", "act", "dve", "pool", "sp")


class Tok:
    __slots__ = ("kind", "key", "needed", "val")

    def __init__(self, kind, key):
        self.kind = kind
        self.key = key
        self.needed = False
        self.val = 0


class Buf:
    __slots__ = ("w", "r")

    def __init__(self):
        self.w = []
        self.r = []


class Sched:
    def __init__(self):
        self.ops = {e: [] for e in ENGS}
        self.dch = {}
        self.cch = {}

    def _mk(self, q, tok, fn, reads, writes, extra):
        deps = list(extra)
        for b in reads:
            deps += b.w
        for b in writes:
            deps += b.w
            deps += b.r
        for d in deps:
            if d.kind == "E" and d.key == q and q == "pe":
                continue
            d.needed = True
        self.ops[q].append((fn, deps, tok))
        for b in reads:
            if tok.kind == "E":
                b.r = [t for t in b.r if not (t.kind == "E" and t.key == tok.key)]
            b.r.append(tok)
        for b in writes:
            b.w = [tok]
            b.r = []
        return tok

    def op(self, eng, fn, reads=(), writes=(), extra=()):
        return self._mk(eng, Tok("E", eng), fn, reads, writes, extra)

    def dma(self, q, ch, fn, reads=(), writes=(), extra=()):
        t = Tok("D", ch)
        t.needed = True
        self.dch.setdefault(ch, None)
        return self._mk(q, t, fn, reads, writes, extra)

    def cc(self, name, fn, reads=(), writes=(), extra=()):
        t = Tok("C", name)
        t.needed = True
        t.val = 1
        self.cch[name] = None
        return self._mk("pool", t, fn, reads, writes, extra)

    def barrier(self, bufs=()):
        last = []
        for e in ENGS:
            for (_, _, tok) in reversed(self.ops[e]):
                if tok.kind == "E":
                    tok.needed = True
                    last.append(tok)
                    break
        dl = {}
        for e in ENGS:
            for (_, _, tok) in self.ops[e]:
                if tok.kind == "D":
                    dl[tok.key] = tok
        self._barrier_deps = last + list(dl.values())
        return self._barrier_deps

    def assign(self):
        for e in ENGS:
            c = 0
            for (_, _, tok) in self.ops[e]:
                if tok.kind == "E" and tok.needed:
                    c += 1
                    tok.val = c
        dc = {}
        for e in ENGS:
            for (_, _, tok) in self.ops[e]:
                if tok.kind == "D":
                    dc[tok.key] = dc.get(tok.key, 0) + 16
                    tok.val = dc[tok.key]

    def emit(self, eng, engobj, esem, dsem, csem=None):
        waited = {}
        for fn, deps, tok in self.ops[eng]:
            need = {}
            for d in deps:
                if d.kind == "E" and d.key == eng and eng == "pe":
                    continue
                k = (d.kind, d.key)
                if need.get(k, 0) < d.val:
                    need[k] = d.val
            for k, v in need.items():
                if waited.get(k, 0) < v:
                    engobj.wait_ge(esem[k[1]] if k[0] == "E" else (dsem[k[1]] if k[0] == "D" else csem[k[1]]), v)
                    waited[k] = v
            ins = fn(engobj)
            if tok.kind == "D":
                ins.then_inc(dsem[tok.key], 16)
            elif tok.kind == "C":
                ins.then_inc(csem[tok.key])
            elif tok.needed:
                ins.then_inc(esem[eng], 1)


class Arena:
    def __init__(self, t, nbytes):
        self.t = t
        self.n = nbytes
        self.off = 0

    def alloc(self, free, dtype):
        n = 1
        for f in free:
            n *= f
        size = n * (4 if dtype == F32 else 2)
        size = (size + 63) // 64 * 64
        st = self.off
        self.off += size
        assert self.off <= self.n, f"arena overflow {self.off} > {self.n}"
        ap = self.t[:, st // 2:(st + n * (4 if dtype == F32 else 2)) // 2]
        if dtype == F32:
            ap = ap.bitcast(F32)
        if len(free) == 2:
            ap = ap.rearrange("p (a b) -> p a b", a=free[0], b=free[1])
        elif len(free) == 3:
            ap = ap.rearrange("p (a b c) -> p a b c", a=free[0], b=free[1], c=free[2])
        return ap

    def alloc_at(self, off, free, dtype):
        keep = self.off
        self.off = off
        ap = self.alloc(free, dtype)
        end = self.off
        self.off = keep
        return ap, end

    def reset(self):
        self.off = 0


class Builder:
    def __init__(self, phases, fused):
        self.phases = phases
        self.fused = fused
        self.nc = bass.Bass("TRN2", target_bir_lowering=False)
        self.S = Sched()
        self.dram = {}
        self.def_cc = {}
        self.wpre = {}
        self.wo_pre = {}

    def din(self, name, shape, dtype):
        if name not in self.dram:
            self.dram[name] = self.nc.dram_tensor(name, list(shape), dtype, kind="ExternalInput").ap()
        return self.dram[name]

    def dout(self, name, shape, dtype):
        if name not in self.dram:
            self.dram[name] = self.nc.dram_tensor(name, list(shape), dtype, kind="ExternalOutput").ap()
        return self.dram[name]

    def dint(self, name, shape, dtype):
        if name not in self.dram:
            self.dram[name] = self.nc.dram_tensor(name, list(shape), dtype).ap()
        return self.dram[name]

    def norm_block(self, hblk, hbuf, gain_cols, out_tile, out_buf, tmp, eps=RMS_EPS):
        S = self.S
        sq, sqb, ssb, ss_bank, rstd, rstdb, ones = (tmp["sq"], tmp["sqb"], tmp["ssb"], tmp["ss"],
                                                      tmp["rstd"], tmp["rstdb"], tmp["ones"])
        for kc in range(8):
            S.op("act", lambda e, kc=kc: e.activation(out=sq[:, kc, :], in_=hblk[:, kc, :], func=AF.Square),
                 reads=[hbuf], writes=[sqb[kc]])
        for kc in range(8):
            S.op("pe", lambda e, kc=kc: e.matmul(ss_bank, ones, sq[:, kc, :], start=(kc == 0), stop=(kc == 7)),
                 reads=[sqb[kc], self.ctx["cbuf"]], writes=[ssb])
        S.op("act", lambda e: e.activation(out=rstd, in_=ss_bank, func=AF.Ln, scale=1.0 / D, bias=tmp["eps"]),
             reads=[ssb, self.ctx["gbuf"]], writes=[rstdb])
        S.op("act", lambda e: e.activation(out=rstd, in_=rstd, func=AF.Exp, scale=-0.5),
             reads=[rstdb], writes=[rstdb])
        for kc in range(8):
            S.op("dve", lambda e, kc=kc: e.scalar_tensor_tensor(
                out=out_tile[:, kc, :], in0=hblk[:, kc, :], scalar=gain_cols[:, kc:kc + 1], in1=rstd,
                op0=ALU.mult, op1=ALU.mult), reads=[hbuf, rstdb, self.ctx["gbuf"]], writes=[out_buf])

    def build(self):
        nc = self.nc
        S = self.S
        phases = self.phases
        fused = self.fused
        gains = self.din("gains", [128, 48], F32)
        consts_bf = self.din("consts_bf", [128, 128 + 128 + MA_W + MB_W], BF16)
        if "T0" in phases:
            xT = self.din("xT", [D, TQ], F32)
        need_A = any(p.startswith("A") for p in phases)
        need_T = any(p in ("T1", "T2") for p in phases)
        if need_A:
            w_in = self.din("w_in", [DEPTH, D, 1024], F32)
            self.w_in_ap = w_in
            ropeC = self.din("ropeC", [128, SEQ], F32)
            ropeS = self.din("ropeS", [128, SEQ], F32)
            lamv = self.din("lamv", [128, DEPTH * 4 * 64], F32)
        if need_T:
            w_out = self.din("w_out", [DEPTH, D, D], F32)
            self.w_out_ap = w_out
            w_gate = self.din("w_gate", [DEPTH, D, D], F32)
            w_ple = self.din("w_ple", [DEPTH, 256, D], F32)
            pT = self.din("pT", [DEPTH, 256, TQ], F32)
        if fused:
            self.xp = [[self.dint(f"xp{l}_{tb}", [D, BLK], BF16) for tb in range(4)] for l in range(DEPTH)]
            self.xg = [[self.dint(f"xg{l}_{tb}", [4 * D, BLK], BF16) for tb in range(4)] for l in range(DEPTH)]
            self.xpb = [[Buf() for tb in range(4)] for l in range(DEPTH)]
            self.xgb = [[Buf() for tb in range(4)] for l in range(DEPTH)]
            self.yp = [[self.dint(f"yp{l}_{sb}", [256, TQ], BF16) for sb in range(4)] for l in range(DEPTH)]
            self.yf = [self.dint(f"yf{l}", [32, 128, TQ], BF16) for l in range(DEPTH)]
            self.ypb = [[Buf() for sb in range(4)] for l in range(DEPTH)]
            self.yfb = [[Buf() for sb in range(4)] for l in range(DEPTH)]
            self.h_sp = self.dint("h_spill", [D, TQ], F32)
            self.hspb = [Buf() for _ in range(4)]
        outT = None

        with (
            nc.sbuf_tensor("arena", [128, 103 * 1024], BF16) as arena_t,
            nc.psum_tensor("psall", [128, 8, 512], F32) as psall_t,
        ):
            psall = psall_t[:, :, :]
            banks = [psall_t[:, k, :] for k in range(8)]
            bankb = [Buf() for _ in range(8)]
            ar = Arena(arena_t, 206 * 1024)
            gains_sb = ar.alloc([48], F32)
            gbuf = Buf()
            cbf = ar.alloc([128 + 128 + MA_W + MB_W], BF16)
            cbuf = Buf()
            eps_t = ar.alloc([2], F32)
            S.dma("sp", "c0", lambda e: e.dma_start(out=gains_sb, in_=gains), writes=[gbuf])
            S.dma("sp", "c1", lambda e: e.dma_start(out=cbf, in_=consts_bf), writes=[cbuf])
            S.op("dve", lambda e: e.memset(eps_t[:, 0:1], RMS_EPS), writes=[gbuf])
            S.op("dve", lambda e: e.memset(eps_t[:, 1:2], SUBLN_EPS), writes=[gbuf])
            ones = cbf[:, 0:128]
            swapm = cbf[:, 128:256]
            MA = cbf[:, 256:256 + MA_W]
            MB = cbf[:, 256 + MA_W:256 + MA_W + MB_W]
            base_off = ar.off
            self.ctx = dict(psall=psall, banks=banks, bankb=bankb, ar=ar, gains_sb=gains_sb, gbuf=gbuf, cbuf=cbuf,
                            ones=ones, swapm=swapm, MA=MA, MB=MB, eps_t=eps_t)

            first = True
            for ph in phases:
                ar.off = base_off
                if not first:
                    self.phase_barrier()
                first = False
                if ph == "T0":
                    self.phase_T0(xT)
                elif ph in ("A1", "A2"):
                    l = int(ph[1]) - 1
                    self.phase_A(l, w_in, ropeC, ropeS, lamv)
                elif ph in ("T1", "T2"):
                    l = int(ph[1]) - 1
                    self.phase_T(l, w_out, w_gate, w_ple, pT)

            S.assign()
            import contextlib
            with contextlib.ExitStack() as st:
                esem = {e: st.enter_context(nc.semaphore("es_" + e)) for e in ENGS}
                dsem = {ch: st.enter_context(nc.semaphore("ds_" + ch)) for ch in S.dch}
                csem = {ch: st.enter_context(nc.semaphore("cs_" + ch)) for ch in S.cch}
                block = st.enter_context(nc.Block())
                final = []
                for e in ENGS:
                    for (_, _, tok) in S.ops[e]:
                        if tok.kind == "D":
                            final.append(tok)
                lastd = {}
                for t in final:
                    lastd[t.key] = t

                @block.tensor
                def _(eng):
                    S.emit("pe", eng, esem, dsem, csem)

                @block.scalar
                def _(eng):
                    S.emit("act", eng, esem, dsem, csem)

                @block.vector
                def _(eng):
                    S.emit("dve", eng, esem, dsem, csem)

                @block.gpsimd
                def _(eng):
                    S.emit("pool", eng, esem, dsem, csem)

                @block.sync
                def _(eng):
                    S.emit("sp", eng, esem, dsem, csem)
                    for ch, t in lastd.items():
                        eng.wait_ge(dsem[ch], t.val)
        return nc

    def phase_barrier(self):
        S = self.S
        deps = S.barrier()
        for e in ENGS:
            if e == "sp":
                pass
            self._pending_barrier = deps
        self.bar = deps

    def bdeps(self):
        return getattr(self, "bar", [])

    GROUPS = [[0, 1, 2, 3], [4, 5, 6, 7]]

    def pidj512(self, e):
        if getattr(self, "_pidj512", None) is None:
            self._pidj512 = (e.partition_id() % 4) * BLK
        return self._pidj512

    def emit_xn_out(self, lnext, tb, xt, xb):
        S = self.S
        sl = slice(tb * BLK, (tb + 1) * BLK)
        if not self.fused:
            xov = self.dout("xn_part", [D, TQ], BF16).rearrange("(k p) t -> p k t", p=128)
            S.dma("sp", f"xo{tb % 2}", lambda e: e.dma_start(out=xov[:, :, sl], in_=xt), reads=[xb])
            return
        xp, xg = self.xp[lnext][tb], self.xg[lnext][tb]
        xpb, xgb = self.xpb[lnext][tb], self.xgb[lnext][tb]
        S.dma("sp", f"xo{tb % 2}", lambda e: e.dma_start(out=xp.rearrange("(k p) t -> p k t", p=128), in_=xt),
              reads=[xb], writes=[xpb])
        def issue():
            S.cc(f"xg{lnext}_{tb}", lambda e: e.collective_compute(
                "AllGather", ALU.bypass, replica_groups=self.GROUPS, ins=[xp], outs=[xg]), reads=[xpb], writes=[xgb])
        if tb == 0:
            issue()
        else:
            self.def_cc.setdefault(lnext, {})[tb] = issue

    def emit_h_out(self, tb, h, hbuf):
        S = self.S
        sl = slice(tb * BLK, (tb + 1) * BLK)
        if self.fused:
            hov = self.h_sp.rearrange("(k p) t -> p k t", p=128)
            S.dma("sp", f"ho{tb}", lambda e: e.dma_start(out=hov[:, :, sl], in_=h[:, :, sl]),
                  reads=[hbuf], writes=[self.hspb[tb]])
        else:
            hov = self.dout("h_out", [D, TQ], F32).rearrange("(k p) t -> p k t", p=128)
            S.dma("sp", f"ho{tb}", lambda e: e.dma_start(out=hov[:, :, sl], in_=h[:, :, sl]), reads=[hbuf])

    def phase_T0(self, xT):
        S = self.S
        c = self.ctx
        ar = c["ar"]
        banks, bankb = c["banks"], c["bankb"]
        wpre = ar.alloc([8, 1024], BF16)
        h = ar.alloc([8, TQ], F32)
        hb = [Buf() for _ in range(4)]
        tmp = self.norm_tmp(ar)
        xn = [ar.alloc([8, BLK], BF16) for _ in range(2)]
        xnb = [Buf() for _ in range(2)]
        xTv = xT.rearrange("(k p) t -> p k t", p=128)
        bd = self.bdeps()
        ld = {}

        def load_x(tb, after=()):
            sl = slice(tb * BLK, (tb + 1) * BLK)
            ld[tb] = S.dma("sp", f"hld{tb}", lambda e: e.dma_start(out=h[:, :, sl], in_=xTv[:, :, sl]),
                           writes=[hb[tb]], extra=bd + list(after))

        load_x(0)
        if self.fused:
            self.wpre[0] = Buf()
            S.dma("pool", "w", lambda e: e.dma_start(out=wpre, in_=self.w_in_ap[0].rearrange("(k p) n -> p k n", p=128)),
                  writes=[self.wpre[0]], extra=[ld[0]])
        load_x(1, [ld[0]])
        for tb in range(4):
            sl = slice(tb * BLK, (tb + 1) * BLK)
            self.norm_block(h[:, :, sl], hb[tb], c["gains_sb"][:, 0:8], xn[tb % 2], xnb[tb % 2], tmp)
            self.emit_xn_out(0, tb, xn[tb % 2], xnb[tb % 2])
            if not self.fused:
                self.emit_h_out(tb, h, hb[tb])
            if tb + 2 < 4:
                load_x(tb + 2)

    def norm_tmp(self, ar, bank=7):
        c = self.ctx
        return dict(sq=ar.alloc([8, BLK], BF16), sqb=[Buf() for _ in range(8)], ssb=c["bankb"][bank],
                    ss=c["banks"][bank], rstd=ar.alloc([BLK], F32), rstdb=Buf(), ones=c["ones"],
                    eps=c["eps_t"][:, 0:1])

    def phase_T(self, l, w_out, w_gate, w_ple, pT):
        S = self.S
        c = self.ctx
        ar = c["ar"]
        banks, bankb = c["banks"], c["bankb"]
        bd = self.bdeps()
        last = (l == DEPTH - 1)
        if self.fused and not last:
            wnext = ar.alloc([8, 1024], BF16)
        h = ar.alloc([8, TQ], F32)
        hb = [Buf() for _ in range(4)]
        wo, wo_end = ar.alloc_at(WO_OFF, [8, D], BF16)
        wg = ar.alloc([8, D], BF16)
        wp = ar.alloc([2, D], BF16)
        wob, wgb, wpb = Buf(), Buf(), Buf()
        pt = ar.alloc([2, TQ], BF16)
        ptb = Buf()
        ytile = [ar.alloc([8, BLK], BF16) for _ in range(2)]
        yb = [Buf() for _ in range(2)]
        hn = [ar.alloc([8, BLK], BF16) for _ in range(2)]
        hnb = [Buf(), Buf()]
        tmps = [self.norm_tmp(ar, bank=6), self.norm_tmp(ar, bank=7)]
        th = [ar.alloc([BLK], F32) for _ in range(2)]
        thb = [Buf(), Buf()]
        t2 = [ar.alloc([BLK], F32) for _ in range(2)]
        t2b = [Buf(), Buf()]
        if last:
            xo = [ar.alloc([8, BLK], F32) for _ in range(1)]
            xob = [Buf()]
        else:
            xo = [ar.alloc([8, BLK], BF16) for _ in range(1)]
            xob = [Buf()]
        if self.fused and self.wo_pre.get(l) is not None:
            wob = self.wo_pre[l]
        else:
            S.dma("pool", "wo", lambda e: e.dma_start(out=wo, in_=w_out[l].rearrange("(k p) n -> p k n", p=128)),
                  writes=[wob], extra=bd)
        if self.fused:
            h_src = self.dram["xT"] if l == 0 else self.h_sp
            yv = self.yf[l].rearrange("a p t -> p a t")
        else:
            h_src = self.din("h_in", [D, TQ], F32)
            y_full = self.din("y_full", [4 * 256, TQ], BF16)
            yv = y_full.rearrange("(k p) t -> p k t", p=128)
        hsv = h_src.rearrange("(k p) t -> p k t", p=128)

        def load_y(tb, after=()):
            sl = slice(tb * BLK, (tb + 1) * BLK)
            yt, ybuf = ytile[tb % 2], yb[tb % 2]
            if self.fused:
                return S.dma("sp", f"yl{tb % 2}", lambda e: e.dma_start(
                    out=yt, in_=yv[:, 8 * tb:8 * tb + 8, bass.ds(self.pidj512(e), BLK)]),
                    reads=[self.yfb[l][tb]], writes=[ybuf], extra=bd + list(after))
            else:
                S.dma("sp", f"yl{tb % 2}", lambda e: e.dma_start(out=yt, in_=yv[:, :, sl]), writes=[ybuf], extra=bd)

        def load_h(tb, after=()):
            sl = slice(tb * BLK, (tb + 1) * BLK)
            return S.dma("sp", f"hld{tb}", lambda e: e.dma_start(out=h[:, :, sl], in_=hsv[:, :, sl]),
                         reads=([self.hspb[tb]] if (self.fused and l > 0) else []), writes=[hb[tb]],
                         extra=bd + list(after))

        ty0 = load_y(0)
        th0 = load_h(0)
        first = [t_ for t_ in (ty0, th0) if t_ is not None]
        S.dma("pool", "wg", lambda e: e.dma_start(out=wg, in_=w_gate[l].rearrange("(k p) n -> p k n", p=128)),
              writes=[wgb], extra=bd + first)
        load_y(1, first)
        th1 = load_h(1, first)
        S.dma("pool", "wp", lambda e: e.dma_start(out=wp, in_=w_ple[l].rearrange("(k p) n -> p k n", p=128)),
              writes=[wpb], extra=bd + first)
        S.dma("pool", "pt", lambda e: e.dma_start(out=pt, in_=pT[l].rearrange("(k p) t -> p k t", p=128)),
              writes=[ptb], extra=bd + first)
        th2 = load_h(2, [th1])
        th3 = load_h(3, [th2])
        if self.fused and not last:
            self.wpre[l + 1] = Buf()
            S.dma("pool", "w", lambda e: e.dma_start(out=wnext, in_=self.w_in_ap[l + 1].rearrange("(k p) n -> p k n", p=128)),
                  writes=[self.wpre[l + 1]], extra=bd + [th3])
        assert ar.off <= WO_OFF, ar.off
        if last:
            outT = self.dout("outT", [D, TQ], F32)
            ov = outT.rearrange("(k p) t -> p k t", p=128)
        gs = c["gains_sb"]

        def stA(tb):
            sl = slice(tb * BLK, (tb + 1) * BLK)
            yt, ybuf = ytile[tb % 2], yb[tb % 2]
            for dc in range(8):
                bk = dc % 2
                for kc in range(8):
                    S.op("pe", lambda e, dc=dc, kc=kc, bk=bk: e.matmul(
                        banks[bk], wo[:, kc, dc * 128:(dc + 1) * 128], yt[:, kc, :], start=(kc == 0), stop=(kc == 7)),
                        reads=[wob, ybuf], writes=[bankb[bk]])
                S.op("dve", lambda e, dc=dc, bk=bk: e.tensor_tensor(
                    out=h[:, dc, sl], in0=banks[bk], in1=h[:, dc, sl], op=ALU.add),
                    reads=[bankb[bk]], writes=[hb[tb]])
            if tb + 2 < 4:
                load_y(tb + 2)

        def stB(tb):
            sl = slice(tb * BLK, (tb + 1) * BLK)
            self.norm_block(h[:, :, sl], hb[tb], gs[:, 16 + 8 * l:24 + 8 * l], hn[tb % 2], hnb[tb % 2], tmps[0])

        def stC(tb):
            sl = slice(tb * BLK, (tb + 1) * BLK)
            hnt, hnbuf = hn[tb % 2], hnb[tb % 2]
            for dc in range(8):
                bg, be = 2 + (dc % 2), 4 + (dc % 2)
                tht, thbuf, t2t, t2buf = th[dc % 2], thb[dc % 2], t2[dc % 2], t2b[dc % 2]
                for kc in range(8):
                    S.op("pe", lambda e, dc=dc, kc=kc, bg=bg: e.matmul(
                        banks[bg], wg[:, kc, dc * 128:(dc + 1) * 128], hnt[:, kc, :], start=(kc == 0), stop=(kc == 7)),
                        reads=[wgb, hnbuf], writes=[bankb[bg]])
                for kc in range(2):
                    S.op("pe", lambda e, dc=dc, kc=kc, be=be: e.matmul(
                        banks[be], wp[:, kc, dc * 128:(dc + 1) * 128], pt[:, kc, sl], start=(kc == 0), stop=(kc == 1)),
                        reads=[wpb, ptb], writes=[bankb[be]])
                S.op("act", lambda e, bg=bg, tht=tht: e.activation(out=tht, in_=banks[bg], func=AF.Tanh, scale=0.5),
                     reads=[bankb[bg]], writes=[thbuf])
                S.op("dve", lambda e, be=be, tht=tht, t2t=t2t: e.scalar_tensor_tensor(
                    out=t2t, in0=tht, scalar=1.0, in1=banks[be], op0=ALU.add, op1=ALU.mult),
                    reads=[bankb[be], thbuf], writes=[t2buf])
                S.op("dve", lambda e, dc=dc, t2t=t2t: e.scalar_tensor_tensor(
                    out=h[:, dc, sl], in0=t2t, scalar=0.5, in1=h[:, dc, sl], op0=ALU.mult, op1=ALU.add),
                    reads=[t2buf], writes=[hb[tb]])

        def stD(tb):
            sl = slice(tb * BLK, (tb + 1) * BLK)
            gcols = gs[:, 32:40] if last else gs[:, 8 * (l + 1):8 * (l + 2)]
            xt, xb = xo[tb % len(xo)], xob[tb % len(xo)]
            self.norm_block(h[:, :, sl], hb[tb], gcols, xt, xb, tmps[1])
            if last:
                S.dma("sp", f"xo{tb % len(xo)}", lambda e: e.dma_start(out=ov[:, :, sl], in_=xt), reads=[xb])
            else:
                self.emit_xn_out(l + 1, tb, xt, xb)
                self.emit_h_out(tb, h, hb[tb])

        order = ["A0", "A1", "B0", "A2", "B1", "C0", "A3", "B2", "D0", "C1", "B3", "D1", "C2", "D2", "C3", "D3"]
        fmap = {"A": stA, "B": stB, "C": stC, "D": stD}
        for o in order:
            fmap[o[0]](int(o[1]))

    def phase_A(self, l, w_in, ropeC, ropeS, lamv):
        S = self.S
        c = self.ctx
        ar = c["ar"]
        banks, bankb = c["banks"], c["bankb"]
        ones, swapm, MA, MB = c["ones"], c["swapm"], c["MA"], c["MB"]
        cbuf = c["cbuf"]
        bd = self.bdeps()
        gs = c["gains_sb"]
        lambda_init = 0.8 - 0.6 * math.exp(-0.3 * l)

        if self.fused:
            xn_full = None
            y_out = None
        else:
            xn_full = self.din("xn_full", [4 * D, TQ], BF16)
            y_out = self.dout("y_part", [256, SEQ], BF16)

        w = ar.alloc([8, 1024], BF16)
        wb = Buf()
        xnt = [ar.alloc([8, BLK], BF16) for _ in range(2)]
        xnb = [Buf(), Buf()]
        KAT = ar.alloc([SEQ], BF16)
        KBT = ar.alloc([SEQ], BF16)
        VA = ar.alloc([32, 192], BF16)
        V4 = ar.alloc([2, 16, 192], BF16)
        V16 = ar.alloc([2, 16, 192], BF16)
        v4b = [Buf(), Buf()]
        v16b = [Buf(), Buf()]
        vscr = self.dint(f"vscr{l}", [SEQ, 128], BF16)
        vsb = [Buf() for _ in range(NBLK)]
        VB = ar.alloc([64, 128], BF16)
        kab = [Buf() for _ in range(NBLK)]
        kbb = [Buf() for _ in range(NBLK)]
        vab = [Buf() for _ in range(NBLK)]
        vbb = [Buf() for _ in range(NBLK)]
        rC = [ar.alloc([BLK], F32) for _ in range(2)]
        rS = [ar.alloc([BLK], F32) for _ in range(2)]
        rb = [Buf(), Buf()]
        QS = [ar.alloc([TQ], BF16) for _ in range(2)]
        qsb = [[Buf() for _ in range(4)] for _ in range(2)]
        qB = ar.alloc([BLK], BF16)
        qbb = Buf()
        GS = [ar.alloc([TQ], BF16) for _ in range(2)]
        gsb = [[Buf() for _ in range(4)] for _ in range(2)]
        gBs = [ar.alloc([BLK], BF16) for _ in range(2)]
        gbbs = [Buf(), Buf()]
        da = ar.alloc([BLK], F32)
        db = ar.alloc([BLK], F32)
        dcc = ar.alloc([BLK], F32)
        dab, dbb, dcb = Buf(), Buf(), Buf()
        qraw = ar.alloc([BLK], BF16)
        qrb = Buf()
        t1 = ar.alloc([BLK], F32)
        t2 = ar.alloc([BLK], F32)
        t1b, t2b = Buf(), Buf()
        NP = 8
        PP = [ar.alloc([2, BLK], BF16) for _ in range(4)]
        P = [PP[k // 2][:, k % 2, :] for k in range(NP)]
        Pb = [Buf() for _ in range(NP)]
        psall = c["psall"]
        PS = [ar.alloc([2, BLK], BF16) for _ in range(2)]
        psb = [Buf(), Buf()]
        ta, tb_ = da, db
        tc = ar.alloc([BLK], F32)
        tab, tbb, tcb = dab, dbb, Buf()
        sqt = ar.alloc([BLK], BF16)
        sqb = Buf()
        YA = ar.alloc([TQ], BF16)
        yasb = Buf()
        yB = [ar.alloc([BLK], BF16) for _ in range(2)]
        ybb = [Buf(), Buf()]
        lam_t = ar.alloc([4 * 64], F32)
        lam_s = ar.alloc([8], F32)
        lamb = Buf()

        if self.fused and self.wpre.get(l) is not None:
            wb = self.wpre[l]
        else:
            S.dma("pool", "w", lambda e: e.dma_start(out=w, in_=w_in[l].rearrange("(k p) n -> p k n", p=128)),
                  writes=[wb], extra=bd)
        S.dma("sp", "lam", lambda e: e.dma_start(out=lam_t, in_=lamv[:, l * 256:(l + 1) * 256]),
              writes=[lamb], extra=bd)
        S.op("pool", lambda e: e.memset(VA[:, :, 64:128], 1.0), writes=vab, extra=bd)
        S.op("pool", lambda e: e.memset(V4[:, :, :, 64:128], 1.0), writes=v4b, extra=bd)
        S.op("pool", lambda e: e.memset(V16[:, :, :, 64:128], 1.0), writes=v16b, extra=bd)
        S.op("dve", lambda e: e.tensor_tensor(out=lam_t[:, 0:64], in0=lam_t[:, 0:64], in1=lam_t[:, 64:128], op=ALU.mult),
             reads=[lamb], writes=[lamb], extra=bd)
        S.op("dve", lambda e: e.tensor_tensor(out=lam_t[:, 128:192], in0=lam_t[:, 128:192], in1=lam_t[:, 192:256], op=ALU.mult),
             reads=[lamb], writes=[lamb])
        S.op("dve", lambda e: e.reduce_sum(out=lam_s[:, 0:1], in_=lam_t[:, 0:64], axis=mybir.AxisListType.X),
             reads=[lamb], writes=[lamb])
        S.op("dve", lambda e: e.reduce_sum(out=lam_s[:, 1:2], in_=lam_t[:, 128:192], axis=mybir.AxisListType.X),
             reads=[lamb], writes=[lamb])
        S.op("act", lambda e: e.activation(out=lam_s[:, 2:4], in_=lam_s[:, 0:2], func=AF.Exp),
             reads=[lamb], writes=[lamb])
        S.op("dve", lambda e: e.tensor_tensor(out=lam_s[:, 4:5], in0=lam_s[:, 3:4], in1=lam_s[:, 2:3], op=ALU.subtract),
             reads=[lamb], writes=[lamb])
        S.op("dve", lambda e: e.tensor_scalar(out=lam_s[:, 4:5], in0=lam_s[:, 4:5], scalar1=-lambda_init, scalar2=None,
                                              op0=ALU.add), reads=[lamb], writes=[lamb])
        S.op("dve", lambda e: e.tensor_scalar(out=lam_s[:, 5:6], in0=gs[:, 40 + l:41 + l], scalar1=(1.0 - lambda_init),
                                              scalar2=None, op0=ALU.mult), reads=[lamb, c["gbuf"]], writes=[lamb])
        neglam = lam_s[:, 4:5]
        sgain = lam_s[:, 5:6]
        eps_sub = c["eps_t"][:, 1:2]

        if self.fused:
            xgv = [g.rearrange("(r k p) t -> p r k t", p=128, k=8) for g in self.xg[l]]
        else:
            xv = xn_full.rearrange("(r k p) t -> p r k t", p=128, k=8)
            yov = y_out
        pi = [0]

        def nextP():
            i = pi[0] % NP
            pi[0] += 1
            return P[i], Pb[i]

        qraws = [qraw, ar.alloc([BLK], BF16)]
        qrbs = [qrb, Buf()]
        t1s = [t1, ar.alloc([BLK], F32)]
        t2s = [t2, ar.alloc([BLK], F32)]
        t1bs = [t1b, Buf()]
        t2bs = [t2b, Buf()]

        def rope_cast(bank_i, u):
            S.op("dve", lambda e: e.tensor_copy(out=qraws[u], in_=banks[bank_i]), reads=[bankb[bank_i]], writes=[qrbs[u]])

        def rope_rest(bank_i, dst, dstbuf, rbi, u):
            bq, bqb = banks[bank_i], bankb[bank_i]
            S.op("pe", lambda e: e.matmul(banks[2], swapm, qraws[u], start=True, stop=True),
                 reads=[qrbs[u], cbuf], writes=[bankb[2]])
            S.op("dve", lambda e: e.tensor_tensor(out=t1s[u], in0=bq, in1=rC[rbi], op=ALU.mult),
                 reads=[bqb, rb[rbi]], writes=[t1bs[u]])
            S.op("dve", lambda e: e.tensor_tensor(out=t2s[u], in0=banks[2], in1=rS[rbi], op=ALU.mult),
                 reads=[bankb[2], rb[rbi]], writes=[t2bs[u]])
            S.op("pool", lambda e: e.tensor_tensor(out=dst, in0=t1s[u], in1=t2s[u], op=ALU.add),
                 reads=[t1bs[u], t2bs[u]], writes=[dstbuf])

        def gate_evac(bank_i, dst, dstbuf):
            bq, bqb = banks[bank_i], bankb[bank_i]
            S.op("act", lambda e: e.activation(out=t1, in_=bq, func=AF.Tanh, scale=0.5), reads=[bqb], writes=[t1b])
            S.op("pool", lambda e: e.tensor_scalar(out=t1, in0=t1, scalar1=0.5, scalar2=0.5, op0=ALU.mult, op1=ALU.add),
                 reads=[t1b], writes=[t1b])
            S.op("dve", lambda e: e.tensor_tensor(out=dst, in0=bq, in1=t1, op=ALU.mult),
                 reads=[bqb, t1b], writes=[dstbuf])

        def emit_loads(i):
            xt, xb = xnt[i % 2], xnb[i % 2]
            rbi = i % 2
            sl = slice(i * BLK, (i + 1) * BLK)
            if self.fused:
                S.dma("sp", f"xn{i % 2}", lambda e: e.dma_start(out=xt, in_=xgv[i // 4][:, i % 4, :, :]),
                      reads=[self.xgb[l][i // 4]], writes=[xb], extra=bd)
            else:
                S.dma("sp", f"xn{i % 2}", lambda e: e.dma_start(
                    out=xt, in_=xv[:, i // 4, :, (i % 4) * BLK:(i % 4 + 1) * BLK]), writes=[xb], extra=bd)
            S.dma("sp", f"rc{rbi}", lambda e: e.dma_start(out=rC[rbi], in_=ropeC[:, sl]), writes=[rb[rbi]], extra=bd)
            S.dma("sp", f"rs{rbi}", lambda e: e.dma_start(out=rS[rbi], in_=ropeS[:, sl]), writes=[rb[rbi]], extra=bd)

        def diff_epilogue(i):
            sl = slice(i * BLK, (i + 1) * BLK)
            sbi, qs = i // 4, slice((i % 4) * BLK, (i % 4 + 1) * BLK)
            O0, O1, D0, D1 = banks[4], banks[5], banks[6], banks[7]
            gB, gbb = gBs[i % 2], gbbs[i % 2]
            yb_, ybbuf = yB[i % 2], ybb[i % 2]
            Dz = banks[6]
            S.op("act", lambda e: e.activation(out=dcc, in_=Dz, func=AF.Ln), reads=[bankb[6]], writes=[dcb])
            S.op("act", lambda e: e.activation(out=dcc, in_=dcc, func=AF.Exp, scale=-1.0), reads=[dcb], writes=[dcb])
            for hf in range(2):
                rr = slice(64 * hf, 64 * hf + 64)
                S.op("dve", lambda e, rr=rr: e.tensor_tensor(out=da[rr, :], in0=O0[rr, :], in1=dcc[0:64, :], op=ALU.mult),
                     reads=[bankb[4], dcb], writes=[dab])
                S.op("dve", lambda e, rr=rr: e.tensor_tensor(out=db[rr, :], in0=O1[rr, :], in1=dcc[64:128, :], op=ALU.mult),
                     reads=[bankb[5], dcb], writes=[dbb])
            S.op("dve", lambda e: e.scalar_tensor_tensor(out=da, in0=db, scalar=neglam, in1=da, op0=ALU.mult, op1=ALU.add),
                 reads=[dbb, dab, lamb], writes=[dab])
            S.op("act", lambda e: e.activation(out=sqt, in_=da, func=AF.Square), reads=[dab], writes=[sqb])

        def diff_epilogue2(i):
            sl = slice(i * BLK, (i + 1) * BLK)
            sbi, qs = i // 4, slice((i % 4) * BLK, (i % 4 + 1) * BLK)
            gB, gbb = gBs[i % 2], gbbs[i % 2]
            yb_, ybbuf = yB[i % 2], ybb[i % 2]
            S.op("pe", lambda e: e.matmul(banks[7], ones, sqt, start=True, stop=True), reads=[sqb, cbuf], writes=[bankb[7]])
            S.op("act", lambda e: e.activation(out=dcc, in_=banks[7], func=AF.Ln, scale=1.0 / 128, bias=eps_sub),
                 reads=[bankb[7], c["gbuf"]], writes=[dcb])
            S.op("act", lambda e: e.activation(out=dcc, in_=dcc, func=AF.Exp, scale=-0.5), reads=[dcb], writes=[dcb])
            S.op("dve", lambda e: e.scalar_tensor_tensor(out=da, in0=da, scalar=sgain, in1=dcc, op0=ALU.mult, op1=ALU.mult),
                 reads=[dab, dcb, lamb], writes=[dab])
            S.op("pool", lambda e: e.tensor_tensor(out=yb_, in0=da, in1=gB, op=ALU.mult),
                 reads=[dab, gbb], writes=[ybbuf])
            if self.fused:
                S.dma("sp", f"yb{i % 2}", lambda e: e.dma_start(out=self.yp[l][sbi][128:256, qs], in_=yb_),
                      reads=[ybbuf], writes=[self.ypb[l][sbi]])
            else:
                S.dma("sp", f"yb{i % 2}", lambda e: e.dma_start(out=yov[128:256, sl], in_=yb_), reads=[ybbuf])

        LT4, UT4 = MA[:, 0:512], MA[:, 512:1024]
        M1 = [MA[:, 1024 + 512 * r:1536 + 512 * r] for r in range(4)]

        def dswa_superblock(n):
            par = n % 2
            t0 = TQ * n
            Q, G = QS[par], GS[par]
            qb_all = qsb[par]

            def class_batches(h, r4):
                hr = slice(64 * h, 64 * h + 64)
                vs = slice(64 * h, 64 * h + 128)
                bl = []
                for half in range(2):
                    tiles = []
                    mms = list(range(8)) if half == 0 else list(range(8, 15))
                    for t_, mm in enumerate(mms):
                        m = 16 * n + mm
                        tiles.append(dict(sc=slice(64 * t_, 64 * t_ + 64), k=KAT[hr, 128 * m:128 * m + 128],
                                          q=Q[hr, 128 * mm + r4:128 * mm + 256:4], kb=[kab[m // 4]],
                                          oc=slice(32 * mm, 32 * mm + 64), v=VA[:, m % 32, vs], vb=[vab[m // 4]]))
                    if half == 1:
                        m = 16 * n + 15
                        tiles.append(dict(sc=slice(448, 480), k=KAT[hr, 128 * m:128 * m + 128],
                                          q=Q[hr, 1920 + r4:2048:4], kb=[kab[m // 4]],
                                          oc=slice(480, 512), v=VA[:, m % 32, vs], vb=[vab[m // 4]]))
                        if n > 0:
                            m = 16 * n - 1
                            tiles.append(dict(sc=slice(480, 512), k=KAT[hr, 128 * m:128 * m + 128],
                                              q=Q[hr, r4:128:4], kb=[kab[m // 4]],
                                              oc=slice(0, 32), v=VA[:, m % 32, vs], vb=[vab[m // 4]]))
                    bl.append((M1[r4], tiles))
                for prev in range(2):
                    tiles = []
                    for j in range(4):
                        jj, nn = (j, n) if not prev else ((j - 1, n) if j > 0 else (3, n - 1))
                        if nn < 0:
                            continue
                        ks = TQ * nn + BLK * jj + r4
                        tiles.append(dict(sc=slice(128 * j, 128 * j + 128), k=KAT[hr, ks:ks + 4 * 127 + 1:4],
                                          q=Q[hr, BLK * j + r4:BLK * (j + 1):4], kb=[kab[4 * nn + jj]],
                                          oc=slice(128 * j, 128 * j + 128), v=V4[:, nn % 2, 4 * r4 + jj, vs],
                                          vb=[v4b[nn % 2]]))
                    bl.append((UT4 if prev else LT4, tiles))
                for prev in range(2):
                    nn = n - prev
                    if nn < 0:
                        continue
                    tiles = []
                    for s_ in range(4):
                        r16 = r4 + 4 * s_
                        ks = TQ * nn + r16
                        tiles.append(dict(sc=slice(128 * s_, 128 * s_ + 128), k=KAT[hr, ks:ks + 16 * 127 + 1:16],
                                          q=Q[hr, r16:TQ:16], kb=[kab[4 * nn + q_] for q_ in range(4)],
                                          oc=slice(s_, 512, 4), v=V16[:, nn % 2, r16, vs], vb=[v16b[nn % 2]]))
                    bl.append((UT4 if prev else LT4, tiles))
                return bl

            batches = []
            for r4 in range(4):
                b0, b1 = class_batches(0, r4), class_batches(1, r4)
                for bi in range(len(b0)):
                    batches.append(dict(mask=b0[bi][0], tiles=(b0[bi][1], b1[bi][1]), accs=(4 + 2 * (r4 % 2), 5 + 2 * (r4 % 2)),
                                        first=(bi == 0), last=(bi == len(b0) - 1), r4=r4))
            nb_ = len(batches)

            def e_S(b_):
                B = batches[b_]
                pr = b_ % 2
                for T0_, T1_ in zip(*B["tiles"]):
                    for hh_, T in ((0, T0_), (1, T1_)):
                        S.op("pe", lambda e, T=T, hh_=hh_: e.matmul(banks[2 * pr + hh_][:, T["sc"]], T["k"], T["q"],
                                                                 start=True, stop=True),
                             reads=T["kb"] + qb_all, writes=[bankb[2 * pr + hh_]])

            def e_exp(b_):
                B = batches[b_]
                pr = b_ % 2
                S.op("act", lambda e: e.activation(out=PP[pr], in_=psall[:, 2 * pr:2 * pr + 2, :], func=AF.Exp, scale=0.125),
                     reads=[bankb[2 * pr], bankb[2 * pr + 1]], writes=[Pb[2 * pr], Pb[2 * pr + 1]])
                for hh_ in range(2):
                    S.op("dve", lambda e, hh_=hh_: e.tensor_tensor(out=PP[pr][:, hh_, :], in0=PP[pr][:, hh_, :], in1=B["mask"],
                                                                 op=ALU.mult), reads=[Pb[2 * pr + hh_], cbuf], writes=[Pb[2 * pr + hh_]])

            def e_PV(b_):
                B = batches[b_]
                pr = b_ % 2
                nt = len(B["tiles"][0])
                if B["first"]:
                    for hh_ in range(2):
                        acc, accb = banks[B["accs"][hh_]], bankb[B["accs"][hh_]]
                        S.op("pe", lambda e, acc=acc: e.matmul(acc, MB[:, 0:128], MA[:, 0:512], start=True, stop=False),
                             reads=[cbuf], writes=[accb])
                for ti, (T0_, T1_) in enumerate(zip(*B["tiles"])):
                    for hh_, T in ((0, T0_), (1, T1_)):
                        acc, accb = banks[B["accs"][hh_]], bankb[B["accs"][hh_]]
                        S.op("pe", lambda e, T=T, ti=ti, hh_=hh_, acc=acc: e.matmul(
                            acc[:, T["oc"]], T["v"], PP[pr][:, hh_, T["sc"]], start=False,
                            stop=(B["last"] and ti == nt - 1)), reads=T["vb"] + [Pb[2 * pr + hh_]], writes=[accb])
                if B["last"]:
                    r4 = B["r4"]
                    for h in range(2):
                        acc, accb = banks[B["accs"][h]], bankb[B["accs"][h]]
                        tA, tAb, tB, tBb = (ta, tab, tb_, tbb) if h == 0 else (tc, tcb, dcc, dcb)
                        ro = slice(64 * h, 64 * h + 64)
                        rd = slice(64 * (1 - h), 64 * (1 - h) + 64)
                        S.op("act", lambda e, acc=acc, rd=rd, tA=tA: e.activation(out=tA[rd, :], in_=acc[rd, :], func=AF.Ln),
                             reads=[accb], writes=[tAb])
                        S.op("act", lambda e, rd=rd, tA=tA: e.activation(out=tA[rd, :], in_=tA[rd, :], func=AF.Exp, scale=-1.0),
                             reads=[tAb], writes=[tAb])
                        S.op("dve", lambda e, acc=acc, ro=ro, rd=rd, tA=tA, tB=tB: e.tensor_tensor(
                            out=tB[ro, :], in0=acc[ro, :], in1=tA[rd, :], op=ALU.mult), reads=[accb, tAb], writes=[tBb])
                        S.op("pool", lambda e, ro=ro, tB=tB: e.tensor_tensor(out=YA[ro, r4:TQ:4], in0=tB[ro, :], in1=G[ro, r4:TQ:4],
                                                                           op=ALU.mult), reads=[tBb] + gsb[par], writes=[yasb])

            e_S(0)
            if nb_ > 1:
                e_S(1)
            for b_ in range(nb_):
                e_exp(b_)
                if b_ + 2 < nb_:
                    e_S(b_ + 2)
                e_PV(b_)
            if self.fused:
                S.dma("sp", "ya", lambda e: e.dma_start(out=self.yp[l][n][0:128, :], in_=YA),
                      reads=[yasb], writes=[self.ypb[l][n]])
                S.cc(f"yg{l}_{n}", lambda e: e.collective_compute(
                    "AllGather", ALU.bypass, replica_groups=self.GROUPS, ins=[self.yp[l][n]],
                    outs=[self.yf[l][8 * n:8 * n + 8].rearrange("k p t -> (k p) t")]),
                    reads=[self.ypb[l][n]], writes=[self.yfb[l][n]])
            else:
                S.dma("sp", "ya", lambda e: e.dma_start(out=yov[0:128, t0:t0 + TQ], in_=YA), reads=[yasb])

        emit_loads(0)
        for i in range(NBLK):
            xt, xb = xnt[i % 2], xnb[i % 2]
            rbi = i % 2
            sl = slice(i * BLK, (i + 1) * BLK)
            if self.fused and i in (1, 5, 9):
                self.def_cc[l][(i + 3) // 4]()
            if self.fused and i == 10:
                wo_t, _ = ar.alloc_at(WO_OFF, [8, D], BF16)
                self.wo_pre[l] = Buf()
                S.dma("pool", "wo", lambda e: e.dma_start(out=wo_t, in_=self.w_out_ap[l].rearrange("(k p) n -> p k n", p=128)),
                      writes=[self.wo_pre[l]], extra=bd)
            sq_ = slice((i % 4) * BLK, (i % 4 + 1) * BLK)
            spar = (i // 4) % 2

            def proj_group(gc, bk):
                for kc in range(8):
                    S.op("pe", lambda e, kc=kc, xt=xt: e.matmul(
                        banks[bk], w[:, kc, gc * 128:(gc + 1) * 128], xt[:, kc, :], start=(kc == 0), stop=(kc == 7)),
                        reads=[wb, xb], writes=[bankb[bk]])

            def v_group(t, bk):
                kt = 4 * i + t
                for kc in range(8):
                    S.op("pe", lambda e, kc=kc, xt=xt: e.matmul(
                        banks[bk][:, 0:256], xt[:, kc, t * 128:(t + 1) * 128], w[:, kc, 768:1024],
                        start=(kc == 0), stop=(kc == 7)), reads=[wb, xb], writes=[bankb[bk]])
                S.op("dve", lambda e: e.tensor_copy(out=VA[:, kt % 32, 0:64], in_=banks[bk][:, 0:64]),
                     reads=[bankb[bk]], writes=[vab[i]])
                S.op("dve", lambda e: e.tensor_copy(out=VA[:, kt % 32, 128:192], in_=banks[bk][:, 64:128]),
                     reads=[bankb[bk]], writes=[vab[i]])
                S.op("dve", lambda e: e.tensor_copy(out=VB[:, kt, :], in_=banks[bk][:, 128:256]),
                     reads=[bankb[bk]], writes=[vbb[i]])

            ropes = [(0, 0, QS[spar][:, sq_], qsb[spar][i % 4]), (1, 1, KAT[:, sl], kab[i]),
                     (3, 3, qB, qbb), (4, 0, KBT[:, sl], kbb[i])]
            for u_, (gc, bk, dst, dstb) in enumerate(ropes):
                proj_group(gc, bk)
                rope_cast(bk, u_ % 2)
                if u_ > 0:
                    pg, pb_, pd, pdb = ropes[u_ - 1]
                    rope_rest(pb_, pd, pdb, rbi, (u_ - 1) % 2)
            if i > 0:
                diff_epilogue(i - 1)
            v_group(0, 1)
            pg, pb_, pd, pdb = ropes[3]
            rope_rest(pb_, pd, pdb, rbi, 1)
            v_group(1, 3)
            v_group(2, 0)
            v_group(3, 1)
            if i > 0:
                diff_epilogue2(i - 1)
            proj_group(2, 3)
            gate_evac(3, GS[spar][:, sq_], gsb[spar][i % 4])
            proj_group(5, 0)
            gate_evac(0, gBs[i % 2], gbbs[i % 2])

            s0 = (4 * i) % 32
            for hh_ in range(2):
                S.dma("sp", f"vo{hh_}", lambda e, s0=s0, i=i, hh_=hh_: e.dma_start(
                    out=vscr[BLK * i:BLK * (i + 1), 64 * hh_:64 * hh_ + 64].rearrange("(t p) c -> p t c", p=128),
                    in_=VA[:, s0:s0 + 4, 128 * hh_:128 * hh_ + 64]), reads=[vab[i]], writes=[vsb[i]])
            if i % 4 == 3:
                n_sb = i // 4
                vpar = n_sb % 2
                for hh_ in range(2):
                    src16 = vscr[TQ * n_sb:TQ * (n_sb + 1), 64 * hh_:64 * hh_ + 64].rearrange("(p r) c -> p r c", r=16)
                    S.dma("sp", f"v16_{vpar}", lambda e, vpar=vpar, src16=src16, hh_=hh_: e.dma_start(
                        out=V16[:, vpar, :, 128 * hh_:128 * hh_ + 64], in_=src16),
                        reads=[vsb[4 * n_sb + q_] for q_ in range(4)], writes=[v16b[vpar]])
                    for j in range(4):
                        r0 = TQ * n_sb + BLK * j
                        src4 = vscr[r0:r0 + BLK, 64 * hh_:64 * hh_ + 64].rearrange("(p r) c -> p r c", r=4)
                        S.dma("sp", f"v4_{vpar}", lambda e, vpar=vpar, src4=src4, j=j, hh_=hh_: e.dma_start(
                            out=V4[:, vpar, j::4, 128 * hh_:128 * hh_ + 64], in_=src4),
                            reads=[vsb[4 * n_sb + j]], writes=[v4b[vpar]])

            if i + 1 < NBLK:
                emit_loads(i + 1)

            if i % 4 == 0 and i > 0:
                dswa_superblock(i // 4 - 1)

            O0, O1, D0, D1 = banks[4], banks[5], banks[6], banks[7]
            nk = 4 * i + 4

            def d_geo(kt):
                j = kt - 4 * i
                c0 = 128 * max(0, j)
                return j, c0, slice(c0, BLK)

            def d_S(kt):
                par = kt % 2
                j, c0, cs = d_geo(kt)
                ksl = slice(kt * 128, (kt + 1) * 128)
                for comp in range(2):
                    rs_ = slice(64 * comp, 64 * comp + 64)
                    S.op("pe", lambda e, rs_=rs_, comp=comp: e.matmul(
                        banks[2 * par + comp][:, cs], KBT[rs_, ksl], qB[rs_, cs], start=True, stop=True),
                        reads=[kbb[kt // 4], qbb], writes=[bankb[2 * par + comp]])

            def d_exp(kt):
                par = kt % 2
                pq = kt % 4
                j, c0, cs = d_geo(kt)
                S.op("act", lambda e: e.activation(out=PP[pq][:, :, cs], in_=psall[:, 2 * par:2 * par + 2, cs],
                                                   func=AF.Exp, scale=0.125),
                     reads=[bankb[2 * par], bankb[2 * par + 1]], writes=[Pb[2 * pq], Pb[2 * pq + 1]])
                if j >= 0:
                    for comp in range(2):
                        S.op("dve", lambda e, comp=comp: e.tensor_tensor(
                            out=PP[pq][:, comp, c0:c0 + 128], in0=PP[pq][:, comp, c0:c0 + 128], in1=MB[:, 384:512],
                            op=ALU.mult), reads=[Pb[2 * pq + comp], cbuf], writes=[Pb[2 * pq + comp]])

            def d_PV(kt):
                par = kt % 4
                j, c0, cs = d_geo(kt)
                for comp in range(2):
                    Pt, Ptb = P[2 * par + comp], Pb[2 * par + comp]
                    S.op("pe", lambda e, comp=comp, Pt=Pt, nk=nk: e.matmul(
                        banks[4 + comp][:, cs], VB[:, kt, :], Pt[:, cs], start=(kt == 0), stop=(kt == nk - 1)),
                        reads=[vbb[kt // 4], Ptb], writes=[bankb[4 + comp]])

            def d_D(kt):
                par = kt % 2
                pq = kt % 4
                j, c0, cs = d_geo(kt)
                if j >= 0:
                    for comp in range(2):
                        Pt, Ptb = P[2 * pq + comp], Pb[2 * pq + comp]
                        S.op("pe", lambda e, comp=comp, Pt=Pt, nk=nk: e.matmul(
                            banks[6][64 * comp:64 * comp + 64, cs], ones[:, 0:64], Pt[:, cs],
                            start=(kt == 0), stop=(kt == nk - 1), tile_position=(0, 64 * comp)),
                            reads=[cbuf, Ptb], writes=[bankb[6]])
                elif par == 1:
                    q_ = (kt // 2) % 2
                    pa_, pb2 = (kt - 1) % 4, kt % 4
                    S.op("dve", lambda e: e.tensor_tensor(out=PS[q_], in0=PP[pa_], in1=PP[pb2], op=ALU.add),
                         reads=[Pb[2 * pa_], Pb[2 * pa_ + 1], Pb[2 * pb2], Pb[2 * pb2 + 1]], writes=[psb[q_]])

            def d_Dpair(kt):
                q_ = (kt // 2) % 2
                for comp in range(2):
                    S.op("pe", lambda e, comp=comp: e.matmul(
                        banks[6][64 * comp:64 * comp + 64, :], ones[:, 0:64], PS[q_][:, comp, :],
                        start=(kt == 1), stop=False, tile_position=(0, 64 * comp)),
                        reads=[cbuf, psb[q_]], writes=[bankb[6]])

            d_S(0)
            if nk > 1:
                d_S(1)
            for kt in range(nk):
                d_exp(kt)
                if kt + 2 < nk:
                    d_S(kt + 2)
                d_PV(kt)
                if kt >= 2 and (kt - 1) % 2 == 1 and (kt - 1) < 4 * i:
                    d_Dpair(kt - 1)
                d_D(kt)

        diff_epilogue(NBLK - 1)
        diff_epilogue2(NBLK - 1)
        dswa_superblock(NBLK // 4 - 1)


def _tok_idx(j):
    return np.concatenate([np.arange(BLK * (4 * s_ + j), BLK * (4 * s_ + j + 1)) for s_ in range(4)])


def _rope_tables():
    half = 8
    inv = np.power(np.float32(ROPE_THETA), -np.arange(half, dtype=np.float32) * np.float32(2.0 / 16)).astype(np.float32)
    pos = np.arange(SEQ, dtype=np.float32)
    ang = pos[None, :] * inv[:, None]
    cos, sin = np.cos(ang).astype(np.float32), np.sin(ang).astype(np.float32)
    C = np.ones((128, SEQ), np.float32)
    Sg = np.zeros((128, SEQ), np.float32)
    for hb in (0, 64):
        C[hb:hb + 8] = cos
        C[hb + 8:hb + 16] = cos
        Sg[hb:hb + 8] = -sin
        Sg[hb + 8:hb + 16] = sin
    return C, Sg


def _consts_bf():
    ones = np.ones((128, 128), np.float32)
    sw = np.zeros((128, 128), np.float32)
    for m in range(128):
        mm = m % 64
        if mm < 8:
            sw[m + 8, m] = 1.0
        elif mm < 16:
            sw[m - 8, m] = 1.0
    p = np.arange(128)[:, None]
    i_ = np.arange(128)[None, :]
    LT = (i_ >= p).astype(np.float32)
    UT = (p >= i_).astype(np.float32)
    c_ = np.arange(64)[None, :]
    M1 = [((4 * c_ + r - p >= 0) & (4 * c_ + r - p <= 128)).astype(np.float32) for r in range(4)]
    cA = np.concatenate([np.tile(LT, (1, 4)), np.tile(UT, (1, 4))] + [np.tile(m, (1, 8)) for m in M1], axis=1)
    z2 = np.arange(MB_W)[None, :] - 384
    cB = ((z2 - p) >= 0).astype(np.float32)
    return np.concatenate([ones, sw, cA, cB], axis=1).astype(BF)


_PROGS = {}


def _prog(phases, fused):
    key = (tuple(phases), fused)
    if key not in _PROGS:
        b = Builder(list(phases), fused)
        _PROGS[key] = b.build()
    return _PROGS[key]


def _host_prep(x, p, attn_norm_gain, w_in, w_out, lambda_q1, lambda_k1, lambda_q2, lambda_k2, subln_gain,
               ple_norm_gain, w_ple_gate, w_ple, final_norm_gain):
    f = lambda a: np.ascontiguousarray(np.asarray(a, dtype=np.float32))
    x, p = f(x), f(p)
    w_in, w_out, w_ple_gate, w_ple = f(w_in), f(w_out), f(w_ple_gate), f(w_ple)
    C, Sg = _rope_tables()
    cb = _consts_bf()

    def pcol(v):
        return np.asarray(v, np.float32).reshape(8, 128).T

    gains = np.zeros((128, 48), np.float32)
    gains[:, 0:8] = pcol(attn_norm_gain[0])
    gains[:, 8:16] = pcol(attn_norm_gain[1])
    gains[:, 16:24] = pcol(ple_norm_gain[0])
    gains[:, 24:32] = pcol(ple_norm_gain[1])
    gains[:, 32:40] = pcol(final_norm_gain)
    gains[:, 40] = np.asarray(subln_gain[0], np.float32)
    gains[:, 41] = np.asarray(subln_gain[1], np.float32)
    lamv = np.zeros((128, DEPTH * 256), np.float32)
    for l in range(DEPTH):
        for k, v in enumerate((lambda_q1, lambda_k1, lambda_q2, lambda_k2)):
            lamv[:, l * 256 + k * 64:l * 256 + (k + 1) * 64] = np.asarray(v[l], np.float32)[None, :]
    per_core = []
    for c in range(8):
        b, j = c // 4, c % 4
        cols = np.concatenate([np.arange(base + 128 * j, base + 128 * j + 128)
                               for base in (0, 512, 1536, 2048, 2560, 3584, 1024, 3072)])
        rows = np.concatenate([np.concatenate([np.arange(128 * r, 128 * r + 128), np.arange(512 + 128 * r, 512 + 128 * r + 128)])
                               for r in range(4)])
        d = dict(
            gains=gains, consts_bf=cb, ropeC=C, ropeS=Sg, lamv=lamv,
            xT=np.ascontiguousarray(x[b, _tok_idx(j), :].T),
            pT=np.ascontiguousarray(np.transpose(p[:, b, _tok_idx(j), :], (0, 2, 1))),
            w_in=np.ascontiguousarray(w_in[:, :, cols]),
            w_out=np.ascontiguousarray(w_out[:, rows, :]),
            w_gate=w_ple_gate, w_ple=w_ple,
        )
        per_core.append(d)
    return per_core


def _run(phases, fused, in_maps):
    nc = _prog(phases, fused)
    res = run_bass_kernel_spmd(nc, in_maps, core_ids=list(range(8)))
    return res.results


def _sel(d, names):
    return {k: d[k] for k in names}


def kernel_fused(**inputs):
    pc = _host_prep(**inputs)
    names = ["gains", "consts_bf", "xT", "w_in", "ropeC", "ropeS", "lamv", "w_out", "w_gate", "w_ple", "pT"]
    r = _run(["T0", "A1", "T1", "A2", "T2"], True, [_sel(pc[c], names) for c in range(8)])
    out = np.zeros((NB, SEQ, D), np.float32)
    for c in range(8):
        out[c // 4, _tok_idx(c % 4), :] = r[c]["outT"].T
    return out


def kernel(**inputs):
    return kernel_fused(**inputs)
```
